# Optimizing a Trainium2 kernel written in Bass

```python
import math
import jax, jax.numpy as jnp
from jax import lax
import numpy as np

D_MODEL = 1024
BATCH = 16
SEQ = 256
DEPTH = 1
DEC_BATCH = 4
DEC_SEQ = 2048
PAST_LEN = 256

GRID_W = 64
N_FOURIER_GROUPS = 4
FOURIER_GROUP = D_MODEL // 8
D_FOURIER = N_FOURIER_GROUPS * FOURIER_GROUP
N_HEADS = 8
QK_NOPE = D_MODEL // 16
QK_ROPE = D_MODEL // 32
V_DIM = D_MODEL // 16
Q_RANK = D_MODEL // 4
KV_RANK = D_MODEL // 8
D_ATTN = N_HEADS * V_DIM
ROPE_THETA = 10000.0
EPS = 1e-6
Q_BLOCK = 128
SPLITS = (D_FOURIER, D_FOURIER, Q_RANK, KV_RANK, QK_ROPE, D_ATTN, D_MODEL, D_MODEL)
D_IN = 2 * D_FOURIER + Q_RANK + KV_RANK + QK_ROPE + D_ATTN + 2 * D_MODEL

kernel_name = 'fnet_mla_gated_hybrid_dit_step'


def rms_norm(x, g=None):
    xf = x.astype(jnp.float32)
    y = xf * lax.rsqrt(jnp.mean(xf * xf, axis=-1, keepdims=True) + EPS)
    if g is not None:
        y = y * g.astype(jnp.float32)
    return y.astype(x.dtype)


def axial_rope(x):
    n = x.shape[1]
    rows = n // GRID_W
    t = jnp.arange(rows * GRID_W)
    row = (t // GRID_W).astype(jnp.float32)
    col = (t % GRID_W).astype(jnp.float32)
    half = QK_ROPE // 2
    inv = ROPE_THETA ** (-jnp.arange(0, half, 2, dtype=jnp.float32) / half)
    ar = row[:, None] * inv
    ac = col[:, None] * inv
    ang = jnp.concatenate([ar, ar, ac, ac], axis=-1)
    ang = ang.reshape((n,) + (1,) * (x.ndim - 3) + (QK_ROPE,))
    cos, sin = jnp.cos(ang), jnp.sin(ang)
    xf = x.astype(jnp.float32)
    q4 = half // 2
    xr, xc = xf[..., :half], xf[..., half:]
    rot = jnp.concatenate([-xr[..., q4:], xr[..., :q4], -xc[..., q4:], xc[..., :q4]], axis=-1)
    return (xf * cos + rot * sin).astype(x.dtype)


def fourier_mix(u):
    b, n, _ = u.shape
    ug = u.astype(jnp.float32).reshape(b, n, N_FOURIER_GROUPS, FOURIER_GROUP)
    f = jnp.fft.fftn(ug, axes=(1, 3), norm='ortho').real
    return f.reshape(b, n, D_FOURIER).astype(u.dtype)


def mla_queries(cq, q_norm_g, w_uq):
    q = rms_norm(cq, q_norm_g) @ w_uq
    q = q.reshape(cq.shape[0], cq.shape[1], N_HEADS, QK_NOPE + QK_ROPE)
    return q[..., :QK_NOPE], q[..., QK_NOPE:]


def mla_keys_values(ckv_n, w_ukv):
    kv = (ckv_n @ w_ukv).reshape(ckv_n.shape[0], ckv_n.shape[1], N_HEADS, QK_NOPE + V_DIM)
    return kv[..., :QK_NOPE], kv[..., QK_NOPE:]


def mla_attention(q_nope, q_rope, k_nope, k_rope, v):
    b, n, h, _ = q_nope.shape
    nb = n // Q_BLOCK
    scale = (QK_NOPE + QK_ROPE) ** -0.5

    def to_blocks(t):
        return t.reshape((b, nb, Q_BLOCK) + t.shape[2:]).swapaxes(0, 1)

    def block(qs):
        qn, qr = qs
        s = jnp.einsum('bqhd,bkhd->bhqk', qn, k_nope) + jnp.einsum('bqhr,bkr->bhqk', qr, k_rope)
        p = jax.nn.softmax(s.astype(jnp.float32) * scale, axis=-1).astype(v.dtype)
        return jnp.einsum('bhqk,bkhd->bqhd', p, v)

    o = lax.map(block, (to_blocks(q_nope), to_blocks(q_rope)))
    return o.swapaxes(0, 1).reshape(b, n, h * V_DIM)


def layer_in(x, cond, w_ada_l, b_ada_l, w_in_l):
    shift, scale, gate = jnp.split(jax.nn.silu(cond) @ w_ada_l + b_ada_l, 3, axis=-1)
    h = rms_norm(x) * (1 + scale) + shift
    parts = jnp.split(h @ w_in_l, np.cumsum(SPLITS)[:-1], axis=-1)
    return gate, parts


def merge_out(x, gate, u_f, z_f, attn, z_a, g_f, g_a, w_f_out_l, w_a_out_l, w_out_l):
    y_f = (fourier_mix(u_f) * jax.nn.silu(z_f)) @ w_f_out_l
    y_a = (attn * jax.nn.silu(z_a)) @ w_a_out_l
    merged = jax.nn.sigmoid(g_f) * y_f + jax.nn.sigmoid(g_a) * y_a
    return x + gate * (merged @ w_out_l)


def context_layer(x, c_ctx, w_ada_l, b_ada_l, w_in_l, q_norm_g_l, w_uq_l, kv_norm_g_l,
                  w_ukv_l, w_f_out_l, w_a_out_l, w_out_l):
    gate, (u_f, z_f, cq, ckv, k_rope, z_a, g_f, g_a) = layer_in(
        x, c_ctx[None, None, :], w_ada_l, b_ada_l, w_in_l)
    q_nope, q_rope = mla_queries(cq, q_norm_g_l, w_uq_l)
    ckv_n = rms_norm(ckv, kv_norm_g_l)
    k_nope, v = mla_keys_values(ckv_n, w_ukv_l)
    attn = mla_attention(q_nope, q_rope, k_nope, k_rope, v)
    x = merge_out(x, gate, u_f, z_f, attn, z_a, g_f, g_a, w_f_out_l, w_a_out_l, w_out_l)
    return x, ckv_n, k_rope


def latent_layer(x, c, ckv_ctx, krope_ctx, w_ada_l, b_ada_l, w_in_l, q_norm_g_l, w_uq_l,
                 kv_norm_g_l, w_ukv_l, w_f_out_l, w_a_out_l, w_out_l):
    gate, (u_f, z_f, cq, ckv, k_rope, z_a, g_f, g_a) = layer_in(
        x, c[:, None, :], w_ada_l, b_ada_l, w_in_l)
    q_nope, q_rope = mla_queries(cq, q_norm_g_l, w_uq_l)
    q_rope = axial_rope(q_rope)
    k_nope_lat, v_lat = mla_keys_values(rms_norm(ckv, kv_norm_g_l), w_ukv_l)
    k_nope_ctx, v_ctx = mla_keys_values(ckv_ctx, w_ukv_l)
    k_nope = jnp.concatenate([k_nope_ctx, k_nope_lat], axis=1)
    k_rope_all = jnp.concatenate([krope_ctx, axial_rope(k_rope)], axis=1)
    v = jnp.concatenate([v_ctx, v_lat], axis=1)
    attn = mla_attention(q_nope, q_rope, k_nope, k_rope_all, v)
    return merge_out(x, gate, u_f, z_f, attn, z_a, g_f, g_a, w_f_out_l, w_a_out_l, w_out_l)


def setup_inputs(seed: int = 0) -> dict:
    key = jax.random.key(seed)
    ks = jax.random.split(key, 20)
    f32 = jnp.float32

    def nrm(k, shape, scale=1.0):
        return jax.random.normal(k, shape, dtype=f32) * scale

    return {
        'x_prompt': nrm(ks[0], (BATCH, SEQ, D_MODEL)),
        'x_sample': nrm(ks[1], (DEC_BATCH, DEC_SEQ, D_MODEL)),
        'cache_ckv': nrm(ks[2], (DEC_BATCH, DEPTH, PAST_LEN, KV_RANK)),
        'cache_krope': nrm(ks[3], (DEC_BATCH, DEPTH, PAST_LEN, QK_ROPE)),
        'c': nrm(ks[4], (DEC_BATCH, D_MODEL)),
        'c_ctx': nrm(ks[5], (D_MODEL,)),
        'w_ada': nrm(ks[6], (DEPTH, D_MODEL, 3 * D_MODEL), 0.5 * D_MODEL ** -0.5),
        'b_ada': nrm(ks[7], (DEPTH, 3 * D_MODEL), 0.02),
        'w_in': nrm(ks[8], (DEPTH, D_MODEL, D_IN), D_MODEL ** -0.5),
        'q_norm_g': 1.0 + nrm(ks[9], (DEPTH, Q_RANK), 0.02),
        'w_uq': nrm(ks[10], (DEPTH, Q_RANK, N_HEADS * (QK_NOPE + QK_ROPE)), Q_RANK ** -0.5),
        'kv_norm_g': 1.0 + nrm(ks[11], (DEPTH, KV_RANK), 0.02),
        'w_ukv': nrm(ks[12], (DEPTH, KV_RANK, N_HEADS * (QK_NOPE + V_DIM)), KV_RANK ** -0.5),
        'w_f_out': nrm(ks[13], (DEPTH, D_FOURIER, D_MODEL), D_FOURIER ** -0.5),
        'w_a_out': nrm(ks[14], (DEPTH, D_ATTN, D_MODEL), D_ATTN ** -0.5),
        'w_out': nrm(ks[15], (DEPTH, D_MODEL, D_MODEL), D_MODEL ** -0.5),
        'final_norm_g': 1.0 + nrm(ks[16], (D_MODEL,), 0.02),
    }


def reference(x_prompt, x_sample, cache_ckv, cache_krope, c, c_ctx, w_ada, b_ada, w_in,
              q_norm_g, w_uq, kv_norm_g, w_ukv, w_f_out, w_a_out, w_out, final_norm_g):
    xp = x_prompt
    ckv_states = []
    krope_states = []
    for l in range(DEPTH):
        xp, ckv_n, k_rope = context_layer(
            xp, c_ctx, w_ada[l], b_ada[l], w_in[l], q_norm_g[l], w_uq[l], kv_norm_g[l],
            w_ukv[l], w_f_out[l], w_a_out[l], w_out[l])
        ckv_states.append(ckv_n)
        krope_states.append(k_rope)
    y_prompt = rms_norm(xp, final_norm_g)
    state_ckv = jnp.stack(ckv_states, axis=1)
    state_krope = jnp.stack(krope_states, axis=1)

    xs = x_sample
    for l in range(DEPTH):
        xs = latent_layer(
            xs, c, cache_ckv[:, l], cache_krope[:, l], w_ada[l], b_ada[l], w_in[l],
            q_norm_g[l], w_uq[l], kv_norm_g[l], w_ukv[l], w_f_out[l], w_a_out[l], w_out[l])
    y_sample = rms_norm(xs, final_norm_g)
    return (y_prompt, y_sample, state_ckv, state_krope)
```

```python
import bisect
import numpy as np
import ml_dtypes
import concourse.bass as bass
import concourse.mybir as mybir
from concourse.bass_utils import run_bass_kernel_spmd

F32 = mybir.dt.float32
BF16 = mybir.dt.bfloat16
ALU = mybir.AluOpType
AF = mybir.ActivationFunctionType

D = 1024
D_IN = 4000
NCORES = 8
EPS = 1e-6
SM_SCALE = 96.0 ** -0.5
NL = {'sp': 24, 'pool': 40}
STOP_AFTER = None


class Tok:
    __slots__ = ('eng', 'idx', 'dma', 'lane', 'val', 'signal')

    def __init__(self, eng, idx, dma):
        self.eng = eng
        self.idx = idx
        self.dma = dma
        self.lane = None
        self.val = 0
        self.signal = False


class IMap:
    def __init__(self, size):
        self.b = [0, size]
        self.w = [None]
        self.r = [{}]

    def _split(self, x):
        i = bisect.bisect_right(self.b, x) - 1
        if self.b[i] == x:
            return
        self.b.insert(i + 1, x)
        self.w.insert(i + 1, self.w[i])
        self.r.insert(i + 1, dict(self.r[i]))

    def span(self, lo, hi):
        self._split(lo)
        self._split(hi)
        return bisect.bisect_left(self.b, lo), bisect.bisect_left(self.b, hi)

    def collect(self, lo, hi, write, deps, soft=None):
        i0, i1 = self.span(lo, hi)
        for i in range(i0, i1):
            if self.w[i] is not None:
                (soft if (write and soft is not None) else deps).add(self.w[i])
            if write:
                (soft if soft is not None else deps).update(self.r[i].values())

    def register(self, lo, hi, write, tok, key):
        i0, i1 = self.span(lo, hi)
        for i in range(i0, i1):
            if write:
                self.w[i] = tok
                self.r[i] = {}
            else:
                self.r[i][key] = tok


class Sched:
    def __init__(self, nc):
        self.nc = nc
        self.ops = {e: [] for e in ('pe', 'act', 'dve', 'pool', 'sp')}
        self.imap = {'sb': IMap(1 << 20), 'ps': IMap(8 * 2048)}
        self.reg = {}
        self.lane_last = {q: [None] * n for q, n in NL.items()}
        self.lane_cnt = {q: [0] * n for q, n in NL.items()}
        self.lane_ctr = {q: 0 for q in NL}
        self.stores = []
        self.order = []

    def rng(self, ap):
        space, base = self.reg[ap.tensor.name]
        if space == 'dram':
            return None
        es = mybir.dt.size(ap.dtype)
        dims = ap.ap
        ps = dims[0][0]
        off = ap.offset % ps if ps > 0 else ap.offset
        lo = hi = off
        for st, cnt in dims[1:]:
            ext = (cnt - 1) * st
            if ext < 0:
                lo += ext
            else:
                hi += ext
        if space == 'ps':
            b0 = base + (lo * es) // 2048
            b1 = base + (hi * es) // 2048
            return 'ps', b0 * 2048, (b1 + 1) * 2048
        return 'sb', base + lo * es, base + (hi + 1) * es

    def add(self, eng, fn, reads=(), writes=(), dma=False, store=False):
        tok = Tok(eng, len(self.ops[eng]), dma)
        deps = set()
        rr = [r for r in (self.rng(a) for a in reads) if r is not None]
        ww = [r for r in (self.rng(a) for a in writes) if r is not None]
        soft = set()
        for sp, lo, hi in rr:
            self.imap[sp].collect(lo, hi, sp == 'ps', deps)
        for sp, lo, hi in ww:
            self.imap[sp].collect(lo, hi, True, deps, soft if sp == 'sb' else None)
        deps |= soft
        key = ('d', eng, tok.idx) if dma else eng
        for sp, lo, hi in rr:
            self.imap[sp].register(lo, hi, sp == 'ps', tok, key)
        for sp, lo, hi in ww:
            self.imap[sp].register(lo, hi, True, tok, key)
        if dma:
            ln = self.lane_ctr[eng] % NL[eng]
            self.lane_ctr[eng] += 1
            prev = self.lane_last[eng][ln]
            if prev is not None:
                deps.add(prev)
            self.lane_last[eng][ln] = tok
            self.lane_cnt[eng][ln] += 1
            tok.lane = (eng, ln)
            tok.val = 16 * self.lane_cnt[eng][ln]
            if store:
                self.stores.append(tok)
        kept = []
        for d in deps:
            if (not d.dma) and d.eng == 'pe' and eng == 'pe' and not dma:
                continue
            kept.append(d)
        self.ops[eng].append((fn, kept, tok))
        self.order.append((eng, len(self.ops[eng]) - 1))
        return tok

    def finalize(self):
        self.ops['sp'].append((None, list(self.stores), Tok('sp', len(self.ops['sp']), False)))
        self.order.append(('sp', len(self.ops['sp']) - 1))
        def key_idx(t):
            return (t.lane, t.val) if t.dma else (t.eng, t.idx)
        floor = {e: {} for e in self.ops}
        vc = {}
        for eng, i in self.order:
            fn, deps, tok = self.ops[eng][i]
            F = floor[eng]
            kept = []
            for d in sorted(deps, key=lambda t: -key_idx(t)[1]):
                k, v = key_idx(d)
                if F.get(k, -1) >= v:
                    continue
                kept.append(d)
                for k2, v2 in vc[id(d)].items():
                    if F.get(k2, -1) < v2:
                        F[k2] = v2
                if F.get(k, -1) < v:
                    F[k] = v
            for d in kept:
                if not d.dma:
                    d.signal = True
            self.ops[eng][i] = (fn, kept, tok)
            c = dict(F)
            k, v = key_idx(tok)
            c[k] = v
            vc[id(tok)] = c
        for e, lst in self.ops.items():
            c = 0
            for fn, deps, tok in lst:
                if not tok.dma and tok.signal:
                    c += 1
                    tok.val = c

    def emit(self, eng, e, esem, lsem):
        floor = {}

        def semof(t):
            return lsem[t.lane] if t.dma else esem[t.eng]

        for fn, deps, tok in self.ops[eng]:
            need = {}
            for d in deps:
                s = semof(d)
                if need.get(s, 0) < d.val:
                    need[s] = d.val
            for s, v in need.items():
                if floor.get(s, 0) < v:
                    e.wait_ge(s, v)
                    floor[s] = v
            if fn is None:
                continue
            ins = fn(e)
            if tok.dma:
                ins.then_inc(semof(tok), 16)
            elif tok.signal:
                ins.then_inc(semof(tok), 1)


def build_program(debug_taps=None):
    nc = bass.Bass("TRN2", target_bir_lowering=False)
    S = Sched(nc)
    dram = {}

    def din(name, shape, dt=F32):
        t = nc.dram_tensor(name, list(shape), dt, kind="ExternalInput")
        S.reg[t.name] = ('dram', 0)
        dram[name] = t.ap()
        return dram[name]

    def dout(name, shape, dt=F32):
        t = nc.dram_tensor(name, list(shape), dt, kind="ExternalOutput")
        S.reg[t.name] = ('dram', 0)
        dram[name] = t.ap()
        return dram[name]

    xcat = din("xcat", [2560, D])
    cckv = din("cckv", [256, 128])
    ckro = din("ckro", [256, 32])
    condT = din("condT", [128, 8, 2])
    b_adaT = din("b_adaT", [128, 24])
    b_ada = din("b_ada", [3072])
    w_ada = din("w_ada", [D, 3072])
    w_in = din("w_in", [D, D_IN])
    qng = din("qng", [256])
    kvg = din("kvg", [128])
    w_uq = din("w_uq128", [256, 1024])
    w_ukv = din("w_ukv", [128, 1024])
    w_f_out = din("w_f_out", [512, D])
    w_a_out = din("w_a_out", [512, D])
    w_out = din("w_out", [D, D])
    fng = din("fng", [D])
    identd = din("ident", [128, 128], BF16)
    dft128d = din("dft128", [128, 256], BF16)
    dftsd = din("dfts", [2, 2, 128, 8, 512], BF16)
    dftcd = din("dftc", [1, 1024], BF16)
    dftpd = din("dftp", [128, 2, 2, 256], BF16)
    ropekd = din("ropek", [128, 16, 2, 32])
    ropeqd = din("ropeq", [32, 2, 1024])

    y_out = dout("y", [1536, D])
    stc_out = dout("stc", [512, 128])
    stk_out = dout("stk", [512, 32])

    base0 = (nc.sbuf_base + 31) // 32 * 32
    limit = nc.sbuf_top - base0

    class Region:
        def __init__(self, lo, hi):
            self.lo, self.hi, self.cur = lo, hi, lo

        def alloc(self, name, shape, dt):
            sz = int(np.prod(shape[1:])) * mybir.dt.size(dt)
            sz = (sz + 31) // 32 * 32
            assert self.cur + sz <= self.hi, (name, self.cur, sz, self.hi)
            t = nc.alloc_sbuf_tensor_at(name, list(shape), dt, offset=base0 + self.cur)
            S.reg[t.name] = ('sb', self.cur)
            self.cur += sz
            return t

        def reset(self, to=None):
            self.cur = self.lo if to is None else to

    KB = 1024
    PERS_END = 80 * KB
    pers = Region(0, PERS_END)
    RP = Region(PERS_END, limit)

    ident = pers.alloc("ident", [128, 128], BF16)
    dft128 = pers.alloc("dft128s", [128, 256], BF16)
    shiftT = pers.alloc("shiftT", [128, 8, 2], F32)
    scale1T = pers.alloc("scale1T", [128, 8, 2], F32)
    gate_bc = pers.alloc("gate_bc", [128, 2, 1024], F32)
    kvg_bc = pers.alloc("kvg_bc", [128, 128], F32)
    qg_bc = pers.alloc("qg_bc", [128, 256], F32)
    neghalf = pers.alloc("neghalf", [128, 1], F32)
    ropek = pers.alloc("ropeks", [128, 16, 2, 32], F32)
    hT_own = pers.alloc("hT_own", [128, 3, 8, 512], BF16)
    zf_own = pers.alloc("zf_own", [128, 3, 4, 512], BF16)
    ckvT = pers.alloc("ckvT", [128, 2816], BF16)
    krT = pers.alloc("krT", [128, 2816], BF16)
    cqT = pers.alloc("cqT", [128, 2, 1536], BF16)
    za_own = pers.alloc("za_own", [128, 4, 1536], BF16)
    stat = pers.alloc("stat", [128, 64], F32)

    PS2 = []
    PSB = []
    PSBH = []
    for i in range(4):
        t = nc.alloc_psum_tensor(f"ps2_{i}", [128, 1024], F32)
        S.reg[t.name] = ('ps', 2 * i)
        tb = t.bitcast(BF16)
        S.reg[tb.name] = ('ps', 2 * i)
        PS2.append(t)
        for hh in range(2):
            PSB.append(t[:, hh * 512:(hh + 1) * 512])
            PSBH.append(tb[:, hh * 1024:(hh + 1) * 1024])

    stat_ctr = [0]

    def newstat():
        i = stat_ctr[0] % 64
        stat_ctr[0] += 1
        return stat[:, i:i + 1]

    def dma(q, out, in_, store=False):
        return S.add(q, lambda e: e.dma_start(out=out, in_=in_), reads=[in_], writes=[out], dma=True, store=store)

    def mm(out, lhsT, rhs, start, stop):
        return S.add('pe', lambda e: e.matmul(out, lhsT=lhsT, rhs=rhs, start=start, stop=stop),
                     reads=[lhsT, rhs], writes=[out])

    def tr(out, in_):
        return S.add('pe', lambda e: e.transpose(out=out, in_=in_, identity=ident[:]), reads=[in_, ident[:]], writes=[out])

    def act(out, in_, func, scale=1.0, bias=0.0, accum=None):
        rd = [in_]
        if not isinstance(scale, float):
            rd.append(scale)
        if not isinstance(bias, float):
            rd.append(bias)
        wr = [out] + ([accum] if accum is not None else [])
        if accum is not None:
            return S.add('act', lambda e: e.activation(out=out, in_=in_, func=func, scale=scale, bias=bias, accum_out=accum),
                         reads=rd, writes=wr)
        return S.add('act', lambda e: e.activation(out=out, in_=in_, func=func, scale=scale, bias=bias), reads=rd, writes=wr)

    def cp(eng, out, in_):
        if eng == 'act':
            return S.add('act', lambda e: e.copy(out=out, in_=in_), reads=[in_], writes=[out])
        return S.add(eng, lambda e: e.tensor_copy(out=out, in_=in_), reads=[in_], writes=[out])

    def tt(eng, out, in0, in1, op):
        return S.add(eng, lambda e: e.tensor_tensor(out=out, in0=in0, in1=in1, op=op), reads=[in0, in1], writes=[out])

    def ts(eng, out, in0, s1, s2, op0, op1=None):
        rd = [in0] + [s for s in (s1, s2) if s is not None and not isinstance(s, float)]
        if op1 is None:
            return S.add(eng, lambda e: e.tensor_scalar(out=out, in0=in0, scalar1=s1, scalar2=None, op0=op0), reads=rd, writes=[out])
        return S.add(eng, lambda e: e.tensor_scalar(out=out, in0=in0, scalar1=s1, scalar2=s2, op0=op0, op1=op1), reads=rd, writes=[out])

    def stt(out, in0, scalar, in1, op0, op1):
        rd = [in0, in1] + ([] if isinstance(scalar, float) else [scalar])
        return S.add('dve', lambda e: e.scalar_tensor_tensor(out=out, in0=in0, scalar=scalar, in1=in1, op0=op0, op1=op1),
                     reads=rd, writes=[out])

    def recip(out, in_):
        return S.add('dve', lambda e: e.reciprocal(out=out, in_=in_), reads=[in_], writes=[out])

    def memset(eng, out, val):
        return S.add(eng, lambda e: e.memset(out, val), writes=[out])

    def rstd_from_ssq(ssq, n):
        v = newstat()
        ts('pool', v, ssq, 1.0 / n, EPS, ALU.mult, ALU.add)
        r = newstat()
        tt('pool', r, v, neghalf[:], ALU.pow)
        return r

    taps = []

    def tap(name, ap_, shape, dt=F32):
        if debug_taps is None or name not in debug_taps:
            return
        o = dout("dbg_" + name, shape, dt)
        dma('sp', o, ap_, store=True)
        taps.append(name)

    def record():
        RP.reset()
        AB = RP.alloc("AB", [128, 20, 4, 256], BF16)
        wA = RP.alloc("wA", [128, 8, 1952], BF16)
        passA_base = RP.cur
        xt = [RP.alloc(f"xt{i}", [128, 1024], F32) for i in range(3)]
        xn = [RP.alloc(f"xn{i}", [128, 1024], BF16) for i in range(2)]
        hT_tmp = [RP.alloc(f"hT_tmp{i}", [128, 8, 512], BF16) for i in range(2)]
        uT = [RP.alloc(f"uT{i}", [128, 4, 512], BF16) for i in range(2)]
        tht = [RP.alloc(f"tht{i}", [128, 512], BF16) for i in range(2)]
        tmf = [RP.alloc(f"tmf{i}", [128, 416], F32) for i in range(2)]
        ckf = [RP.alloc(f"ckf{i}", [128, 128], F32) for i in range(2)]
        ckb = [RP.alloc(f"ckb{i}", [128, 128], BF16) for i in range(4)]
        kst = [RP.alloc(f"kst{i}", [128, 96], BF16) for i in range(4)]
        cqb = [RP.alloc(f"cqb{i}", [128, 256], BF16) for i in range(4)]
        rt1 = RP.alloc("rt1", [128, 32], F32)
        rt2 = RP.alloc("rt2", [128, 32], F32)
        ccf = RP.alloc("ccf", [128, 2, 128], F32)
        ckrf = RP.alloc("ckrf", [128, 2, 32], F32)


        def prep(k):
            T = k
            x_ = xt[k % 3]
            dma('sp', x_[:], xcat[T * 128:(T + 1) * 128, :])
            ssq = newstat()
            xn_ = xn[k % 2]
            act(xn_[:], x_[:], AF.Square, accum=ssq)
            r = rstd_from_ssq(ssq, 1024)
            ts('dve', xn_[:], x_[:], r, None, ALU.mult)

        P0 = Region(RP.lo, RP.lo + 12 * KB)
        cT = P0.alloc("cT", [128, 8, 2], F32)
        bT = P0.alloc("bT", [128, 24], F32)
        gb = P0.alloc("gb", [128, 1024], F32)
        th0 = P0.alloc("th0", [128, 8, 2], F32)
        sc_bf = P0.alloc("sc_bf", [128, 8, 2], BF16)
        screp = P0.alloc("screp", [128, 8, 2, 128], BF16)
        PG = Region(RP.lo + 24 * KB, RP.lo + 40 * KB)
        wada_g = PG.alloc("wada_g", [128, 8, 1024], BF16)
        PS_ = Region(RP.hi - 32 * KB, RP.hi)
        wada = PS_.alloc("wada", [128, 8, 2048], BF16)

        dma('sp', ident[:], identd)
        dma('sp', dft128[:], dft128d)
        dma('sp', cT[:], condT)
        dma('sp', bT[:], b_adaT)
        dma('sp', gb[:], b_ada[2048:3072].partition_broadcast(128))
        dma('sp', kvg_bc[:], kvg.partition_broadcast(128))
        dma('sp', qg_bc[:], qng.partition_broadcast(128))
        dma('sp', ropek[:], ropekd)
        memset('pool', neghalf[:], -0.5)
        for k in range(8):
            for hh in range(2):
                dma('pool', wada[:, k, hh * 1024:(hh + 1) * 1024], w_ada[k * 128:(k + 1) * 128, hh * 1024:(hh + 1) * 1024])

        prep(0)
        prep(1)
        act(th0[:], cT[:], AF.Tanh, scale=0.5)
        stt(sc_bf[:], th0[:], 1.0, cT[:], ALU.add, ALU.mult)
        cp('dve', screp[:], sc_bf[:].unsqueeze(3).to_broadcast([128, 8, 2, 128]))
        psM = PSB[0]
        memset('dve', psM[:, 0:32], 0.0)
        for k in range(8):
            for j in range(16):
                o_ = psM[:, 2 * j:2 * j + 2]
                l_ = wada[:, k, j * 128:(j + 1) * 128]
                r_ = sc_bf[:, k, :]
                S.add('pe', lambda e, o_=o_, l_=l_, r_=r_, k=k: e.matmul(o_, lhsT=l_, rhs=r_, start=False, stop=(k == 7),
                                                                      skip_group_check=True),
                      reads=[l_, r_], writes=[o_])
        psMv = psM[:, 0:32].rearrange("p (j c) -> p j c", c=2)
        stt(shiftT[:], psMv[:, 0:8, :], 0.5, bT[:, 0:8].unsqueeze(2).to_broadcast([128, 8, 2]), ALU.mult, ALU.add)
        stt(scale1T[:], psMv[:, 8:16, :], 0.5, bT[:, 8:16].unsqueeze(2).to_broadcast([128, 8, 2]), ALU.mult, ALU.add)
        ts('dve', scale1T[:], scale1T[:], 1.0, None, ALU.add)
        ts('dve', gb[:], gb[:], 0.5, None, ALU.mult)

        def gate_stage():
            for c in range(2):
                for hh in range(2):
                    pg = PSB[2 + (c * 2 + hh) % 2]
                    for k in range(8):
                        mm(pg[:, :], screp[:, k, c, :], wada_g[:, k, hh * 512:(hh + 1) * 512], k == 0, k == 7)
                    stt(gate_bc[:, c, hh * 512:(hh + 1) * 512], pg[:, :], 0.25, gb[:, hh * 512:(hh + 1) * 512], ALU.mult, ALU.add)
        tap("shiftT", shiftT[:], [128, 8, 2])
        tap("scale1T", scale1T[:], [128, 8, 2])
        tap("gate_bc", gate_bc[:], [128, 2, 1024])

        if STOP_AFTER == 'phase0':
            return
        for (c0_, c1_) in ((0, 512), (512, 1232), (1232, 1952)):
            for k in range(8):
                dma('pool', wA[:, k, c0_:c1_], w_in[k * 128:(k + 1) * 128, c0_:c1_])
        for i in range(4):
            memset('pool', kst[i][:], 0.0)

        def load_gate_cols():
            for k in range(8):
                dma('pool', wada_g[:, k, :], w_ada[k * 128:(k + 1) * 128, 2048:3072])

        OWN = (0, 1, 2)

        def blk_cond(b):
            return 0 if b == 0 else 1

        def hT_of(b):
            return hT_own[:, b] if b in OWN else hT_tmp[b % 2][:]

        tile_ctr = [0]

        def xpose(k):
            b, t = k // 4, k % 4
            c = blk_cond(b)
            hT = hT_of(b)
            xn_ = xn[k % 2]
            pT = PSBH[(0, 4, 5, 7)[k]] if k < 4 else PSBH[0]
            for j in range(8):
                tr(pT[:, j * 128:(j + 1) * 128], xn_[:, j * 128:(j + 1) * 128])
            for j in range(8):
                o = hT[:, j, t * 128:(t + 1) * 128]
                i_ = pT[:, j * 128:(j + 1) * 128]
                if k % 2 == 0 and j < 4:
                    act(o, i_, AF.Identity, scale=scale1T[:, j, c:c + 1], bias=shiftT[:, j, c:c + 1])
                else:
                    ts('dve', o, i_, scale1T[:, j, c:c + 1], shiftT[:, j, c:c + 1], ALU.mult, ALU.add)

        def stage1_units(b):
            def unit(t):
                k = 4 * b + t
                xpose(k)
                if k + 2 < 20:
                    prep(k + 2)
            return [lambda t=t: unit(t) for t in range(4)]

        mmbank = [0]

        def proj_fm(hT, col0, evac):
            p = PSB[1 + mmbank[0] % 3]
            mmbank[0] += 1
            for k in range(8):
                mm(p[:, :], wA[:, k, col0:col0 + 128], hT[:, k, :], k == 0, k == 7)
            evac(p)

        thc = [0]

        def silu2_evac(out):
            def f(p):
                act(out, p[:, :], AF.Silu)
            return f

        stg = [0]
        tm_info = {}
        tmc = [0]

        def stage2_units(b):
            hT = hT_of(b)
            own = b in OWN
            u = uT[b % 2]
            units = []
            for g in range(4):
                units.append(lambda g=g: proj_fm(hT, g * 128, lambda p: cp('act', u[:, g, :], p[:, :])))
            if own:
                for j in range(4):
                    units.append(lambda j=j: proj_fm(hT, 512 + j * 128, silu2_evac(zf_own[:, b, j, :])))
                for j in range(4):
                    units.append(lambda j=j: proj_fm(hT, 1440 + j * 128, silu2_evac(za_own[:, j, b * 512:(b + 1) * 512])))

            def tm_unit(t):
                T = 4 * b + t
                p = PSB[6]
                if own:
                    c0, ncol, oq, ock, okr = 1024, 416, 0, 256, 384
                else:
                    c0, ncol, oq, ock, okr = 1280, 160, None, 0, 128
                for k in range(8):
                    mm(p[:, 0:ncol], hT[:, k, t * 128:(t + 1) * 128], wA[:, k, c0:c0 + ncol], k == 0, k == 7)
                f_ = tmf[tmc[0] % 2]
                tmc[0] += 1
                cp('act', f_[:, 0:ncol], p[:, 0:ncol])
                si = stg[0] % 4
                stg[0] += 1
                s2 = newstat()
                act(ckb[si][:], f_[:, ock:ock + 128], AF.Square, accum=s2)
                r2 = rstd_from_ssq(s2, 128)
                if b == 0:
                    cf = ckf[si % 2]
                    stt(cf[:], f_[:, ock:ock + 128], r2, kvg_bc[:], ALU.mult, ALU.mult)
                    cp('pool', ckb[si][:], cf[:])
                    dma('pool', stc_out[T * 128:(T + 1) * 128, :], cf[:], store=True)
                else:
                    stt(ckb[si][:], f_[:, ock:ock + 128], r2, kvg_bc[:], ALU.mult, ALU.mult)
                if b == 0:
                    cp('pool', kst[si][:, 64:96], f_[:, okr:okr + 32])
                    dma('pool', stk_out[T * 128:(T + 1) * 128, :], f_[:, okr:okr + 32], store=True)
                else:
                    ch = T - 4
                    pk = f_[:, okr:okr + 32]
                    tt('pool', rt1[:], pk, ropek[:, ch, 0, :], ALU.mult)
                    pk3 = pk.rearrange("p (a b c) -> p a b c", a=2, b=2)
                    sn3 = ropek[:, ch, 1, :].rearrange("p (a b c) -> p a b c", a=2, b=2)
                    r23 = rt2[:].rearrange("p (a b c) -> p a b c", a=2, b=2)
                    tt('pool', r23[:, :, 0, :], pk3[:, :, 1, :], sn3[:, :, 0, :], ALU.mult)
                    tt('pool', r23[:, :, 1, :], pk3[:, :, 0, :], sn3[:, :, 1, :], ALU.mult)
                    tt('pool', kst[si][:, 64:96], rt1[:], rt2[:], ALU.add)
                if own:
                    s3 = newstat()
                    act(cqb[si][:], f_[:, oq:oq + 256], AF.Square, accum=s3)
                    r3 = rstd_from_ssq(s3, 256)
                    stt(cqb[si][:], f_[:, oq:oq + 256], r3, qg_bc[:], ALU.mult, ALU.mult)
                tm_info[(b, t)] = si
            for t in range(4):
                units.append(lambda t=t: tm_unit(t))
            return units

        def key_chunk(b, t):
            return 4 * b + t

        def stage3_units(b):
            own = b in OWN
            u = uT[b % 2]

            def unit(t):
                T = 4 * b + t
                for half in range(2):
                    p = PSB[4 + half]
                    for gg in range(2):
                        g = half * 2 + gg
                        mm(p[:, gg * 256:(gg + 1) * 256], u[:, g, t * 128:(t + 1) * 128], dft128[:, :], True, True)
                    cp('dve' if half == 0 else 'act', AB[:, T, 2 * half:2 * half + 2, :],
                       p[:, :].rearrange("p (g m) -> p g m", g=2))
                si = tm_info[(b, t)]
                kc = key_chunk(b, t)
                pt = PSBH[7]
                tr(pt[:, 0:128], ckb[si][:])
                tr(pt[0:96, 128:256], kst[si][:])
                if own:
                    for r_ in range(2):
                        tr(pt[:, 256 + r_ * 128:384 + r_ * 128], cqb[si][:, r_ * 128:(r_ + 1) * 128])
                cp('dve', ckvT[:, kc * 128:(kc + 1) * 128], pt[:, 0:128])
                cp('dve', krT[64:96, kc * 128:(kc + 1) * 128], pt[64:96, 128:256])
                if own:
                    cp('dve', cqT[:, :, T * 128:(T + 1) * 128], pt[:, 256:512].rearrange("p (r t) -> p r t", r=2))
            return [lambda t=t: unit(t) for t in range(4)]

        def cache_stage():
            dma('sp', ccf[:], cckv.rearrange("(c p) r -> p c r", p=128))
            dma('sp', ckrf[:], ckro.rearrange("(c p) r -> p c r", p=128))
            for c_ in range(2):
                si = stg[0] % 4
                stg[0] += 1
                cp('pool', ckb[si][:], ccf[:, c_, :])
                cp('pool', kst[si][:, 64:96], ckrf[:, c_, :])
                pt = PSBH[7]
                tr(pt[:, 0:128], ckb[si][:])
                tr(pt[0:96, 128:256], kst[si][:])
                kc = 20 + c_
                cp('dve', ckvT[:, kc * 128:(kc + 1) * 128], pt[:, 0:128])
                cp('act', krT[64:96, kc * 128:(kc + 1) * 128], pt[64:96, 128:256])

        NB = 5
        for i in range(NB + 2):
            if i == 1:
                load_gate_cols()
            if i == 2:
                gate_stage()
            a_units = stage1_units(i) if i < NB else []
            s3 = stage3_units(i - 2) if 0 <= i - 2 < NB else []
            s2 = stage2_units(i - 1) if 0 <= i - 1 < NB else []
            tmu = s2[-4:] if s2 else []
            pj = s2[:-4] if s2 else []
            b_units = []
            npj = len(pj)
            for t in range(4):
                if s3:
                    b_units.append(s3[t])
                lo_, hi_ = (t * npj) // 4, ((t + 1) * npj) // 4
                b_units += pj[lo_:hi_]
                if tmu and t >= 1:
                    b_units.append(tmu[t - 1])
            if tmu:
                b_units.append(tmu[3])
            na, nb_ = len(a_units), len(b_units)
            if na == 0:
                for u_ in b_units:
                    u_()
            else:
                per = nb_ / na
                done = 0
                for ai, au in enumerate(a_units):
                    au()
                    upto = int(round((ai + 1) * per))
                    while done < upto:
                        b_units[done]()
                        done += 1
                while done < nb_:
                    b_units[done]()
                    done += 1
        cache_stage()
        tap("hT0", hT_own[:, 0], [128, 8, 512], BF16)
        tap("ckvT", ckvT[:], [128, 2816], BF16)
        tap("krT", krT[64:96, :], [32, 2816], BF16)
        tap("cqT", cqT[:], [128, 2, 1536], BF16)
        tap("zf", zf_own[:], [128, 3, 4, 512], BF16)
        tap("za", za_own[:], [128, 4, 1536], BF16)
        tap("AB", AB[:], [128, 20, 4, 256], BF16)

        if STOP_AFTER == 'passA':
            return
        RP.reset(passA_base)
        Ts0 = RP.alloc("Ts0", [128, 2, 8, 512], BF16)
        Tp = RP.alloc("Tp", [128, 2, 2, 256], BF16)
        Ts1 = RP.alloc("Ts1", [128, 2, 8, 512], BF16)
        corr = RP.alloc("corr", [1, 1024], BF16)
        Ts = [Ts0, Ts1]
        four_end = RP.cur
        AW = Region(RP.hi - 16 * KB, RP.hi)
        assert AW.lo >= four_end
        wq = AW.alloc("wq", [128, 2, 1024], BF16)
        wkv = AW.alloc("wkv", [128, 1024], BF16)
        ropeq = AW.alloc("ropeqs", [128, 2, 1024], F32)
        for ab in range(2):
            dma('sp', Ts[0][:, ab], dftsd[0, ab])
        dma('sp', Tp[:], dftpd)
        for ab in range(2):
            dma('sp', Ts[1][:, ab], dftsd[1, ab])
        dma('sp', corr[:], dftcd)
        for c_ in range(8):
            own_ = AB[:, 4 + c_].rearrange("p g (ab m) -> p g ab m", ab=2)
            oth_ = AB[:, 12 + c_].rearrange("p g (ab m) -> p g ab m", ab=2)
            if c_ == 0:
                pass
            tt('dve' if c_ % 2 == 0 else 'pool', own_[:, :, 0, :], own_[:, :, 0, :], oth_[:, :, 0, :], ALU.add)
            tt('dve' if c_ % 2 == 0 else 'pool', own_[:, :, 1, :], own_[:, :, 1, :], oth_[:, :, 1, :], ALU.subtract)
        for r_ in range(2):
            dma('pool', wq[:, r_, :], w_uq[r_ * 128:(r_ + 1) * 128, :])
        dma('pool', wkv[:], w_ukv)
        dma('sp', ropeq[64:96, 0, :], ropeqd[:, 0, :])
        dma('sp', ropeq[96:128, 1, :], ropeqd[:, 1, :])
        fb = [0]
        for s_ in range(2):
            for g in range(4):
                p = PSB[4 + fb[0] % 4]
                fb[0] += 1
                n = 0
                for c_ in range(2):
                    for ab in range(2):
                        mm(p[:, 0:256], AB[:, 2 * s_ + c_, g, ab * 128:(ab + 1) * 128], Tp[:, c_, ab, :], n == 0, n == 3)
                        n += 1
                o = zf_own[:, 0, g, s_ * 256:(s_ + 1) * 256]
                tt('dve', o, p[:, 0:256], o, ALU.mult)
        for kb in range(2):
            for ab in range(2):
                for g in range(4):
                    p = PSB[4 * kb + g]
                    for c_ in range(8):
                        mm(p[:, :], AB[:, 4 + c_, g, ab * 128:(ab + 1) * 128], Ts[kb][:, ab, c_, :],
                           ab == 0 and c_ == 0, False)
            for g in range(4):
                p = PSB[4 * kb + g]
                mm(p[:, :], AB[0:1, 12, g, 0:128], corr[0:1, kb * 512:(kb + 1) * 512], False, True)
                o = zf_own[:, 1 + kb, g, :]
                tt('dve', o, p[:, :], o, ALU.mult)
        tap("fmz", zf_own[:], [128, 3, 4, 512], BF16)

        if STOP_AFTER == 'fourier':
            return
        RP.reset()
        KT = [RP.alloc(f"KT{i}", [128, 2304], BF16) for i in range(2)]
        Vh = [RP.alloc(f"Vh{i}", [128, 18, 128], BF16) for i in range(2)]
        QT = [RP.alloc(f"QT{i}", [128, 1024], BF16) for i in range(2)]
        PT = [RP.alloc(f"PT{i}", [128, 2, 512], BF16) for i in range(3)]
        qr1 = RP.alloc("qr1", [128, 512], F32)
        qr2 = RP.alloc("qr2", [128, 512], F32)
        QX = nc.alloc_sbuf_tensor_at("QX", [128, 2048], BF16, offset=base0 + S.reg[qr1.name][1])
        S.reg[QX.name] = ('sb', S.reg[qr1.name][1])
        assert S.reg[qr2.name][1] == S.reg[qr1.name][1] + 2048
        rc = [RP.alloc(f"rc{i}", [128, 512], F32) for i in range(2)]
        on_ = [RP.alloc(f"on{i}", [128, 512], F32) for i in range(2)]
        att_end = RP.cur
        MT_END = att_end + 2 * KB
        RP.reset(MT_END)
        wC = RP.alloc("wC", [128, 8, 2048], BF16)
        wf = RP.alloc("wf", [128, 4, 1024], BF16)
        wa = RP.alloc("wa", [128, 4, 1024], BF16)
        wo = RP.alloc("wo", [128, 8, 1024], BF16)
        fng_bc = RP.alloc("fng_bc", [128, 1024], F32)
        assert RP.cur <= AW.lo, (RP.cur, AW.lo)
        for k in range(8):
            for hh in range(2):
                dma('pool', wC[:, k, hh * 1024:(hh + 1) * 1024], w_in[k * 128:(k + 1) * 128, 1952 + hh * 1024:1952 + (hh + 1) * 1024])
        for g in range(4):
            dma('pool', wf[:, g, :], w_f_out[g * 128:(g + 1) * 128, :])
            dma('pool', wa[:, g, :], w_a_out[g * 128:(g + 1) * 128, :])
        for k in range(8):
            dma('pool', wo[:, k, :], w_out[k * 128:(k + 1) * 128, :])
        dma('sp', fng_bc[:], fng.partition_broadcast(128))

        slot_ctr = [0]
        obk = [0]

        ub = [0]

        def attn_setup_units(h, key_chunks, seqs, rope, st):
            sl = slot_ctr[0] % 2
            slot_ctr[0] += 1
            if rope:
                kt, vh, qt = KT[sl], Vh[sl], QT[sl]
            else:
                kt = KT[h // 4][:, (h % 4) * 512:(h % 4 + 1) * 512]
                vh = Vh[h // 4][:, (h % 4) * 4:(h % 4 + 1) * 4, :]
                if h < 2:
                    qt = QT[0][:, h * 512:(h + 1) * 512]
                elif h < 6:
                    qt = QX[:, (h - 2) * 512:(h - 1) * 512]
                else:
                    qt = QT[1][:, (h - 6) * 512:(h - 5) * 512]
            odd = h % 2
            nk = len(key_chunks)
            kc0 = key_chunks[0]
            ncols = nk * 128
            q_lo = min(s_[2] for s_ in seqs)
            q_hi = max(s_[2] + s_[3] for s_ in seqs)
            st.update(kt=kt, vh=vh, qt=qt, odd=odd, q_lo=q_lo, h=h)
            units = []
            vo = 64 if odd else 0
            oo = 0 if odd else 64

            def u0():
                cp('dve', kt[64:96, 0:ncols], krT[64:96, kc0 * 128:kc0 * 128 + ncols])
                if st['gi'] < 2 or not rope:
                    memset('pool', vh[:, :, oo:oo + 64], 1.0)
            if st['gi'] in (0, 1) or not rope:
                units.append(u0)

            def uk(c0):
                n = min(512, ncols - c0)
                p = PSB[ub[0] % 2]
                ub[0] += 1
                mm(p[:, 0:n], wkv[:, h * 128:(h + 1) * 128], ckvT[:, kc0 * 128 + c0:kc0 * 128 + c0 + n], True, True)
                cp('dve', kt[0:64, c0:c0 + n], p[0:64, 0:n])
            for c0 in range(0, ncols, 512):
                units.append(lambda c0=c0: uk(c0))

            def uv(c0):
                n = min(8, nk - c0)
                p = PSB[ub[0] % 2]
                ub[0] += 1
                for c_ in range(n):
                    mm(p[:, c_ * 64:(c_ + 1) * 64], ckvT[:, (kc0 + c0 + c_) * 128:(kc0 + c0 + c_ + 1) * 128],
                       wkv[:, h * 128 + 64:h * 128 + 128], True, True)
                cp('dve', vh[:, c0:c0 + n, vo:vo + 64], p[:, 0:n * 64].rearrange("p (c d) -> p c d", d=64))
            for c0 in range(0, nk, 8):
                units.append(lambda c0=c0: uv(c0))

            def uq(c0):
                n = min(512, q_hi - c0)
                lo = c0 - q_lo
                p = PSB[ub[0] % 2]
                ub[0] += 1
                if rope:
                    for r_ in range(2):
                        mm(p[:, 0:n], wq[:, r_, h * 128:(h + 1) * 128], cqT[:, r_, c0:c0 + n], r_ == 0, r_ == 1)
                    rq0 = c0 - 512
                    cp('dve', qt[0:64, lo:lo + n], p[0:64, 0:n])
                    tt('dve', qr1[64:96, 0:n], p[64:96, 0:n], ropeq[64:96, 0, rq0:rq0 + n], ALU.mult)
                    tt('dve', qr2[64:96, 0:n], p[96:128, 0:n], ropeq[96:128, 1, rq0:rq0 + n], ALU.mult)
                    tt('pool', qt[64:96, lo:lo + n], qr1[64:96, 0:n], qr2[64:96, 0:n], ALU.add)
                else:
                    for r_ in range(2):
                        mm(p[0:96, 0:n], wq[:, r_, h * 128:h * 128 + 96], cqT[:, r_, c0:c0 + n], r_ == 0, r_ == 1)
                    cp('dve', qt[0:96, lo:lo + n], p[0:96, 0:n])
            for c0 in range(q_lo, q_hi, 512):
                units.append(lambda c0=c0: uq(c0))
            return units

        jobs = []
        hooks = {}
        groups = [(h, list(range(4, 22)), [(0, 18, 512, 1024)], True) for h in range(8)] + \
                 [(h, [0, 1, 2, 3], [(0, 2, 0, 256), (2, 4, 256, 256)], False) for h in range(8)]
        states = {gi: {'gi': gi} for gi in range(len(groups))}
        grp_jobs = []
        for gi, (h, key_chunks, seqs, rope) in enumerate(groups):
            first_job = len(jobs)
            for (k_lo, k_hi, q0, nq) in seqs:
                for qb0 in range(0, nq, 512):
                    n = min(512, nq - qb0)
                    og = obk[0]
                    obk[0] += 1
                    for kc in range(k_lo, k_hi, 2):
                        jobs.append(dict(gi=gi, kcs=list(range(kc, min(kc + 2, k_hi))), q0=q0, qb0=qb0, n=n, og=og,
                                         first=(kc == k_lo), last=(kc + 2 >= k_hi)))
            grp_jobs.append((first_job, len(jobs) - first_job))

        def add_hook(ji, f):
            hooks.setdefault(ji, []).append(f)

        first_units = None
        prompt_units = []
        for gi in range(len(groups)):
            units = attn_setup_units(*groups[gi], states[gi])
            if gi == 0:
                first_units = units
                continue
            if not groups[gi][3]:
                prompt_units.append(units)
                continue
            fj, nj = grp_jobs[gi - 1]
            span = max(nj - 1, 1)
            for k_, u_ in enumerate(units):
                ji = fj + min(nj - 1, 1 + (k_ * span) // len(units)) if nj > 1 else fj
                add_hook(ji, u_)

        def issue_S(ji):
            j = jobs[ji]
            st = states[j['gi']]
            bank = PS2[1 + ji % 2]
            ql = j['q0'] - st['q_lo'] + j['qb0']
            n = j['n']
            for i, kc in enumerate(j['kcs']):
                mm(bank[:, i * 512:i * 512 + n], st['kt'][0:96, kc * 128:(kc + 1) * 128], st['qt'][0:96, ql:ql + n], True, True)

        deferred = []

        def issue_rest(ji):
            j = jobs[ji]
            st = states[j['gi']]
            bank = PS2[1 + ji % 2]
            n = j['n']
            L = len(j['kcs'])
            pt_ = PT[ji % 3]
            act(pt_[:, 0:L, 0:n], bank[:, :].rearrange("p (c n) -> p c n", c=2)[:, 0:L, 0:n], AF.Exp, scale=SM_SCALE)
            samp = groups[j['gi']][3]
            po = PSB[6 + j['og'] % 2] if samp else PSB[(6, 7, 0, 1)[j['og'] % 4]]
            for i, kc in enumerate(j['kcs']):
                mm(po[:, 0:n], st['vh'][:, kc, :], pt_[:, i, 0:n], j['first'] and i == 0, j['last'] and i == L - 1)
            if j['last']:
                odd = st['odd']
                h = st['h']
                pb = 64 if odd else 0
                db = 0 if odd else 64
                if samp:
                    rc_ = rc[j['og'] % 2]
                    on = on_[j['og'] % 2]
                else:
                    co = ((j['og'] // 2) % 2) * 256
                    rc_ = rc[j['og'] % 2][:, co:co + 256]
                    on = on_[j['og'] % 2][:, co:co + 256]
                zo = za_own[pb:pb + 64, h // 2, j['q0'] + j['qb0']:j['q0'] + j['qb0'] + n]
                if groups[j['gi']][3]:
                    hn = n // 2
                    deferred.append(lambda: recip(rc_[pb:pb + 64, 0:hn], po[db:db + 64, 0:hn]))
                    deferred.append(lambda: recip(rc_[pb:pb + 64, hn:n], po[db:db + 64, hn:n]))

                    def fin():
                        tt('dve', on[pb:pb + 64, 0:n], po[pb:pb + 64, 0:n], rc_[pb:pb + 64, 0:n], ALU.mult)
                        tt('pool', zo, on[pb:pb + 64, 0:n], zo, ALU.mult)
                    deferred.append(fin)
                else:
                    recip(rc_[pb:pb + 64, 0:n], po[db:db + 64, 0:n])
                    tt('dve', on[pb:pb + 64, 0:n], po[pb:pb + 64, 0:n], rc_[pb:pb + 64, 0:n], ALU.mult)
                    tt('pool', zo, on[pb:pb + 64, 0:n], zo, ALU.mult)

        n_samp = grp_jobs[8][0]

        def run_pipeline(j0, j1):
            issue_S(j0)
            for ji in range(j0, j1):
                for f_ in hooks.get(ji, []):
                    f_()
                if ji + 1 < j1:
                    issue_S(ji + 1)
                if deferred:
                    deferred.pop(0)()
                issue_rest(ji)
            while deferred:
                deferred.pop(0)()

        early_units, late_units = [], []
        for k_ in range(3):
            for h_ in range(8):
                (early_units if h_ < 4 else late_units).append(prompt_units[h_][k_])
        for h_ in range(8):
            (early_units if h_ < 6 else late_units).append(prompt_units[h_][3])
        fj7, nj7 = grp_jobs[7]
        for i_, u_ in enumerate(early_units):
            add_hook(fj7 + 1 + (i_ * (nj7 - 3)) // len(early_units), u_)
        for u_ in first_units:
            u_()
        run_pipeline(0, n_samp)
        for u_ in late_units:
            u_()
        run_pipeline(n_samp, len(jobs))
        tap("attn", za_own[:], [128, 4, 1536], BF16)

        if STOP_AFTER == 'attention':
            return
        MT = Region(RP.lo, MT_END)
        mT = [MT.alloc(f"mT{i}", [128, 8, 512], BF16) for i in range(2)]
        tg = [MT.alloc(f"tg{i}", [128, 512], BF16) for i in range(2)]
        t1 = [MT.alloc(f"t1{i}", [128, 512], F32) for i in range(2)]
        t2 = [MT.alloc(f"t2{i}", [128, 512], F32) for i in range(2)]
        xr = [MT.alloc(f"xr{i}", [128, 1024], F32) for i in range(2)]
        xw = [MT.alloc(f"xw{i}", [128, 1024], F32) for i in range(2)]

        mc = [0]
        for T0 in range(2):
            dma('sp', xr[T0][:], xcat[T0 * 128:(T0 + 1) * 128, :])
        def jiter(b, j):
            m_ = mT[b % 2]
            i_ = mc[0] % 2
            mc[0] += 1
            pg = PSB[0]
            for k in range(8):
                mm(pg[:, :], wC[:, k, j * 128:(j + 1) * 128], hT_own[:, b, k, :], k == 0, k == 7)
            act(tg[0][:], pg[:, :], AF.Tanh, scale=0.5)
            py = PSB[2]
            for g in range(4):
                mm(py[:, :], wf[:, g, j * 128:(j + 1) * 128], zf_own[:, b, g, :], g == 0, g == 3)
            stt(t1[i_][:], tg[0][:], 1.0, py[:, :], ALU.add, ALU.mult)
            pg2 = PSB[1]
            for k in range(8):
                mm(pg2[:, :], wC[:, k, 1024 + j * 128:1024 + (j + 1) * 128], hT_own[:, b, k, :], k == 0, k == 7)
            act(tg[1][:], pg2[:, :], AF.Tanh, scale=0.5)
            py2 = PSB[3]
            for g in range(4):
                mm(py2[:, :], wa[:, g, j * 128:(j + 1) * 128], za_own[:, g, b * 512:(b + 1) * 512], g == 0, g == 3)
            stt(t2[i_][:], tg[1][:], 1.0, py2[:, :], ALU.add, ALU.mult)
            tt('dve', m_[:, j, :], t1[i_][:], t2[i_][:], ALU.add)

        def outproj(b, t):
            c = blk_cond(b)
            m_ = mT[b % 2]
            T = 4 * b + t
            x_ = xr[T % 2]
            w_ = xw[T % 2]
            for hh in range(2):
                po = PSB[4 + 2 * (T % 2) + hh]
                for k in range(8):
                    mm(po[:, :], m_[:, k, t * 128:(t + 1) * 128], wo[:, k, hh * 512:(hh + 1) * 512], k == 0, k == 7)
                tt('dve', w_[:, hh * 512:(hh + 1) * 512], po[:, :], gate_bc[:, c, hh * 512:(hh + 1) * 512], ALU.mult)
            tt('dve', w_[:], w_[:], x_[:], ALU.add)
            ssq = newstat()
            act(x_[:], w_[:], AF.Square, accum=ssq)
            if T + 2 < 12:
                dma('sp', xr[T % 2][:], xcat[(T + 2) * 128:(T + 3) * 128, :])
            r = rstd_from_ssq(ssq, 1024)
            stt(w_[:], w_[:], r, fng_bc[:], ALU.mult, ALU.mult)
            dma('sp', y_out[T * 128:(T + 1) * 128, :], w_[:], store=True)

        for j in range(8):
            jiter(0, j)
        for b in range(3):
            for t in range(4):
                if b + 1 < 3:
                    jiter(b + 1, 2 * t)
                    jiter(b + 1, 2 * t + 1)
                outproj(b, t)

    record()

    S.finalize()
    from contextlib import ExitStack
    with ExitStack() as es:
        esem = {e: es.enter_context(nc.semaphore("s_" + e)) for e in ('pe', 'act', 'dve', 'pool')}
        lsem = {}
        for q, n in NL.items():
            for i in range(n):
                lsem[(q, i)] = es.enter_context(nc.semaphore(f"l_{q}{i}"))
        block = es.enter_context(nc.Block())

        @block.tensor
        def _(e):
            S.emit('pe', e, esem, lsem)

        @block.scalar
        def _(e):
            S.emit('act', e, esem, lsem)

        @block.vector
        def _(e):
            S.emit('dve', e, esem, lsem)

        @block.gpsimd
        def _(e):
            S.emit('pool', e, esem, lsem)

        @block.sync
        def _(e):
            S.emit('sp', e, esem, lsem)
    return nc, taps


def _bf16(a):
    return np.ascontiguousarray(a.astype(np.float32)).astype(ml_dtypes.bfloat16)


_TAB_CACHE = {}


def _tables(hf):
    if hf in _TAB_CACHE:
        return _TAB_CACHE[hf]
    own = np.arange(hf * 1024, hf * 1024 + 1024)
    oth = (2048 - own) % 2048
    oth[0] = 1024 if hf == 0 else 0
    assert sorted(oth.tolist()) == list(range((1 - hf) * 1024, (1 - hf) * 1024 + 1024))
    pos = np.concatenate([own, oth])
    kn = (own[:, None].astype(np.int64) * own[None, :].astype(np.int64)) % 2048
    ang = 2.0 * np.pi * kn.astype(np.float64) / 2048.0
    sc = 1.0 / np.sqrt(2048.0 * 128.0)
    tab = np.stack([np.cos(ang) * sc, -np.sin(ang) * sc])
    tab = tab.reshape(2, 8, 128, 2, 512).transpose(3, 0, 2, 1, 4)
    dfts = _bf16(tab)
    a_o = 2.0 * np.pi * ((int(oth[0]) * own.astype(np.int64)) % 2048) / 2048.0
    a_w = 2.0 * np.pi * ((int(own[0]) * own.astype(np.int64)) % 2048) / 2048.0
    dftc = _bf16(((np.cos(a_o) - np.cos(a_w)) * sc)[None, :])
    half = 16
    inv = 10000.0 ** (-np.arange(0, half, 2, dtype=np.float64) / half)
    row = (pos // 64).astype(np.float64)
    col = (pos % 64).astype(np.float64)
    ar = row[:, None] * inv
    ac = col[:, None] * inv
    ang_r = np.concatenate([ar, ar, ac, ac], axis=-1)
    sgn = np.array([-1.0] * 8 + [1.0] * 8 + [-1.0] * 8 + [1.0] * 8)
    cs = np.stack([np.cos(ang_r), np.sin(ang_r) * sgn], axis=1)
    ropek = cs.reshape(16, 128, 2, 32).transpose(1, 0, 2, 3).astype(np.float32)
    ropeq = cs[:1024].transpose(2, 1, 0).astype(np.float32)
    _TAB_CACHE[hf] = (dfts, np.ascontiguousarray(ropek), np.ascontiguousarray(ropeq), dftc, oth)
    return _TAB_CACHE[hf]


def _const_tables():
    c = np.arange(128)
    ang = 2.0 * np.pi * ((c[:, None] * c[None, :]) % 128) / 128.0
    dft128 = _bf16(np.concatenate([np.cos(ang), np.sin(ang)], axis=1))
    n = np.arange(256)
    angp = 2.0 * np.pi * ((n[:, None] * n[None, :]) % 256) / 256.0
    sc = 1.0 / np.sqrt(256.0 * 128.0)
    tp = np.stack([np.cos(angp) * sc, -np.sin(angp) * sc])
    tp = tp.reshape(2, 2, 128, 256).transpose(2, 1, 0, 3)
    ident = np.eye(128, dtype=np.float32).astype(ml_dtypes.bfloat16)
    return dft128, _bf16(tp), ident


_PROG = {}


def _make_in_maps(x_prompt, x_sample, cache_ckv, cache_krope, c, c_ctx, w_ada, b_ada, w_in,
                  q_norm_g, w_uq, kv_norm_g, w_ukv, w_f_out, w_a_out, w_out, final_norm_g):
    f = lambda a: np.ascontiguousarray(np.asarray(a, dtype=np.float32))
    x_prompt, x_sample = f(x_prompt), f(x_sample)
    dft128, dftp, ident = _const_tables()
    perm = np.array(list(range(8, 16)) + list(range(0, 8)) + list(range(24, 32)) + list(range(16, 24)))
    w_uq0 = f(w_uq)[0]
    w3 = w_uq0.reshape(256, 8, 96)
    w_uq128 = np.ascontiguousarray(np.concatenate([w3, w3[:, :, 64:][:, :, perm]], axis=2).reshape(256, 1024))
    shared = dict(
        b_adaT=np.ascontiguousarray(f(b_ada)[0].reshape(24, 128).T), b_ada=f(b_ada)[0], w_ada=f(w_ada)[0],
        w_in=f(w_in)[0], qng=f(q_norm_g)[0], kvg=f(kv_norm_g)[0], w_uq128=w_uq128, w_ukv=f(w_ukv)[0],
        w_f_out=f(w_f_out)[0], w_a_out=f(w_a_out)[0], w_out=f(w_out)[0], fng=f(final_norm_g),
        ident=ident, dft128=dft128, dftp=dftp)
    in_maps = []
    for i in range(NCORES):
        s, hf = i // 2, i % 2
        dfts, ropek, ropeq, dftc, oth_idx = _tables(hf)
        own = x_sample[s, hf * 1024:(hf + 1) * 1024]
        oth = x_sample[s][oth_idx]
        xcat = np.concatenate([x_prompt[2 * i], x_prompt[2 * i + 1], own, oth], axis=0)
        cond = np.stack([f(c_ctx), f(c)[s]])
        condT = np.ascontiguousarray(cond.reshape(2, 8, 128).transpose(2, 1, 0))
        m = dict(shared)
        m.update(xcat=np.ascontiguousarray(xcat), cckv=f(cache_ckv)[s, 0], ckro=f(cache_krope)[s, 0],
                 condT=condT, dfts=dfts, dftc=dftc, ropek=ropek, ropeq=ropeq)
        in_maps.append(m)
    return in_maps


def kernel(x_prompt, x_sample, cache_ckv, cache_krope, c, c_ctx, w_ada, b_ada, w_in,
           q_norm_g, w_uq, kv_norm_g, w_ukv, w_f_out, w_a_out, w_out, final_norm_g, _debug_taps=None):
    in_maps = _make_in_maps(x_prompt, x_sample, cache_ckv, cache_krope, c, c_ctx, w_ada, b_ada, w_in,
                            q_norm_g, w_uq, kv_norm_g, w_ukv, w_f_out, w_a_out, w_out, final_norm_g)
    nc, taps = build_program(_debug_taps)
    res = run_bass_kernel_spmd(nc, in_maps, core_ids=list(range(NCORES)))
    y_prompt = np.zeros((16, 256, D), np.float32)
    y_sample = np.zeros((4, 2048, D), np.float32)
    st_ckv = np.zeros((16, 1, 256, 128), np.float32)
    st_kr = np.zeros((16, 1, 256, 32), np.float32)
    for i in range(NCORES):
        r = res.results[i]
        s, hf = i // 2, i % 2
        y = np.asarray(r["y"], dtype=np.float32)
        y_prompt[2 * i:2 * i + 2] = y[0:512].reshape(2, 256, D)
        y_sample[s, hf * 1024:(hf + 1) * 1024] = y[512:1536]
        st_ckv[2 * i:2 * i + 2, 0] = np.asarray(r["stc"], dtype=np.float32).reshape(2, 256, 128)
        st_kr[2 * i:2 * i + 2, 0] = np.asarray(r["stk"], dtype=np.float32).reshape(2, 256, 32)
    if _debug_taps is not None:
        return (y_prompt, y_sample, st_ckv, st_kr), [{k: r[k] for k in r if k.startswith("dbg_")} for r in res.results]
    return (y_prompt, y_sample, st_ckv, st_kr)
```

```python
import bisect
import numpy as np
import ml_dtypes
import concourse.bass as bass
import concourse.mybir as mybir
from concourse.bass_utils import run_bass_kernel_spmd

F32 = mybir.dt.float32
BF16 = mybir.dt.bfloat16
ALU = mybir.AluOpType
AF = mybir.ActivationFunctionType

D = 1024
D_IN = 4000
NCORES = 8
EPS = 1e-6
SM_SCALE = 96.0 ** -0.5
NL = {'sp': 24, 'pool': 40}
STOP_AFTER = None


class Tok:
    __slots__ = ('eng', 'idx', 'dma', 'lane', 'val', 'signal')

    def __init__(self, eng, idx, dma):
        self.eng = eng
        self.idx = idx
        self.dma = dma
        self.lane = None
        self.val = 0
        self.signal = False


class IMap:
    def __init__(self, size):
        self.b = [0, size]
        self.w = [None]
        self.r = [{}]

    def _split(self, x):
        i = bisect.bisect_right(self.b, x) - 1
        if self.b[i] == x:
            return
        self.b.insert(i + 1, x)
        self.w.insert(i + 1, self.w[i])
        self.r.insert(i + 1, dict(self.r[i]))

    def span(self, lo, hi):
        self._split(lo)
        self._split(hi)
        return bisect.bisect_left(self.b, lo), bisect.bisect_left(self.b, hi)

    def collect(self, lo, hi, write, deps, soft=None):
        i0, i1 = self.span(lo, hi)
        for i in range(i0, i1):
            if self.w[i] is not None:
                (soft if (write and soft is not None) else deps).add(self.w[i])
            if write:
                (soft if soft is not None else deps).update(self.r[i].values())

    def register(self, lo, hi, write, tok, key):
        i0, i1 = self.span(lo, hi)
        for i in range(i0, i1):
            if write:
                self.w[i] = tok
                self.r[i] = {}
            else:
                self.r[i][key] = tok


class Sched:
    def __init__(self, nc):
        self.nc = nc
        self.ops = {e: [] for e in ('pe', 'act', 'dve', 'pool', 'sp')}
        self.imap = {'sb': IMap(1 << 20), 'ps': IMap(8 * 2048)}
        self.reg = {}
        self.lane_last = {q: [None] * n for q, n in NL.items()}
        self.lane_cnt = {q: [0] * n for q, n in NL.items()}
        self.lane_ctr = {q: 0 for q in NL}
        self.stores = []
        self.order = []

    def rng(self, ap):
        space, base = self.reg[ap.tensor.name]
        if space == 'dram':
            return None
        es = mybir.dt.size(ap.dtype)
        dims = ap.ap
        ps = dims[0][0]
        off = ap.offset % ps if ps > 0 else ap.offset
        lo = hi = off
        for st, cnt in dims[1:]:
            ext = (cnt - 1) * st
            if ext < 0:
                lo += ext
            else:
                hi += ext
        if space == 'ps':
            b0 = base + (lo * es) // 2048
            b1 = base + (hi * es) // 2048
            return 'ps', b0 * 2048, (b1 + 1) * 2048
        return 'sb', base + lo * es, base + (hi + 1) * es

    def add(self, eng, fn, reads=(), writes=(), dma=False, store=False):
        tok = Tok(eng, len(self.ops[eng]), dma)
        deps = set()
        rr = [r for r in (self.rng(a) for a in reads) if r is not None]
        ww = [r for r in (self.rng(a) for a in writes) if r is not None]
        soft = set()
        for sp, lo, hi in rr:
            self.imap[sp].collect(lo, hi, sp == 'ps', deps)
        for sp, lo, hi in ww:
            self.imap[sp].collect(lo, hi, True, deps, soft if sp == 'sb' else None)
        deps |= soft
        key = ('d', eng, tok.idx) if dma else eng
        for sp, lo, hi in rr:
            self.imap[sp].register(lo, hi, sp == 'ps', tok, key)
        for sp, lo, hi in ww:
            self.imap[sp].register(lo, hi, True, tok, key)
        if dma:
            ln = self.lane_ctr[eng] % NL[eng]
            self.lane_ctr[eng] += 1
            prev = self.lane_last[eng][ln]
            if prev is not None:
                deps.add(prev)
            self.lane_last[eng][ln] = tok
            self.lane_cnt[eng][ln] += 1
            tok.lane = (eng, ln)
            tok.val = 16 * self.lane_cnt[eng][ln]
            if store:
                self.stores.append(tok)
        kept = []
        for d in deps:
            if (not d.dma) and d.eng == 'pe' and eng == 'pe' and not dma:
                continue
            kept.append(d)
        self.ops[eng].append((fn, kept, tok))
        self.order.append((eng, len(self.ops[eng]) - 1))
        return tok

    def finalize(self):
        self.ops['sp'].append((None, list(self.stores), Tok('sp', len(self.ops['sp']), False)))
        self.order.append(('sp', len(self.ops['sp']) - 1))
        def key_idx(t):
            return (t.lane, t.val) if t.dma else (t.eng, t.idx)
        floor = {e: {} for e in self.ops}
        vc = {}
        for eng, i in self.order:
            fn, deps, tok = self.ops[eng][i]
            F = floor[eng]
            kept = []
            for d in sorted(deps, key=lambda t: -key_idx(t)[1]):
                k, v = key_idx(d)
                if F.get(k, -1) >= v:
                    continue
                kept.append(d)
                for k2, v2 in vc[id(d)].items():
                    if F.get(k2, -1) < v2:
                        F[k2] = v2
                if F.get(k, -1) < v:
                    F[k] = v
            for d in kept:
                if not d.dma:
                    d.signal = True
            self.ops[eng][i] = (fn, kept, tok)
            c = dict(F)
            k, v = key_idx(tok)
            c[k] = v
            vc[id(tok)] = c
        for e, lst in self.ops.items():
            c = 0
            for fn, deps, tok in lst:
                if not tok.dma and tok.signal:
                    c += 1
                    tok.val = c

    def emit(self, eng, e, esem, lsem):
        floor = {}

        def semof(t):
            return lsem[t.lane] if t.dma else esem[t.eng]

        for fn, deps, tok in self.ops[eng]:
            need = {}
            for d in deps:
                s = semof(d)
                if need.get(s, 0) < d.val:
                    need[s] = d.val
            for s, v in need.items():
                if floor.get(s, 0) < v:
                    e.wait_ge(s, v)
                    floor[s] = v
            if fn is None:
                continue
            ins = fn(e)
            if tok.dma:
                ins.then_inc(semof(tok), 16)
            elif tok.signal:
                ins.then_inc(semof(tok), 1)


def build_program(debug_taps=None):
    nc = bass.Bass("TRN2", target_bir_lowering=False)
    S = Sched(nc)
    dram = {}

    def din(name, shape, dt=F32):
        t = nc.dram_tensor(name, list(shape), dt, kind="ExternalInput")
        S.reg[t.name] = ('dram', 0)
        dram[name] = t.ap()
        return dram[name]

    def dout(name, shape, dt=F32):
        t = nc.dram_tensor(name, list(shape), dt, kind="ExternalOutput")
        S.reg[t.name] = ('dram', 0)
        dram[name] = t.ap()
        return dram[name]

    xcat = din("xcat", [2560, D])
    cckv = din("cckv", [256, 128])
    ckro = din("ckro", [256, 32])
    condT = din("condT", [128, 8, 2])
    b_adaT = din("b_adaT", [128, 24])
    b_ada = din("b_ada", [3072])
    w_ada = din("w_ada", [D, 3072])
    w_in = din("w_in", [D, D_IN])
    qng = din("qng", [256])
    kvg = din("kvg", [128])
    w_uq = din("w_uq128", [256, 1024])
    w_ukv = din("w_ukv", [128, 1024])
    w_f_out = din("w_f_out", [512, D])
    w_a_out = din("w_a_out", [512, D])
    w_out = din("w_out", [D, D])
    fng = din("fng", [D])
    identd = din("ident", [128, 128], BF16)
    dft128d = din("dft128", [128, 256], BF16)
    dftsd = din("dfts", [2, 2, 128, 8, 512], BF16)
    dftcd = din("dftc", [1, 1024], BF16)
    dftpd = din("dftp", [128, 2, 2, 256], BF16)
    ropekd = din("ropek", [128, 16, 2, 32])
    ropeqd = din("ropeq", [32, 2, 1024])

    y_out = dout("y", [1536, D])
    stc_out = dout("stc", [512, 128])
    stk_out = dout("stk", [512, 32])

    base0 = (nc.sbuf_base + 31) // 32 * 32
    limit = nc.sbuf_top - base0

    class Region:
        def __init__(self, lo, hi):
            self.lo, self.hi, self.cur = lo, hi, lo

        def alloc(self, name, shape, dt):
            sz = int(np.prod(shape[1:])) * mybir.dt.size(dt)
            sz = (sz + 31) // 32 * 32
            assert self.cur + sz <= self.hi, (name, self.cur, sz, self.hi)
            t = nc.alloc_sbuf_tensor_at(name, list(shape), dt, offset=base0 + self.cur)
            S.reg[t.name] = ('sb', self.cur)
            self.cur += sz
            return t

        def reset(self, to=None):
            self.cur = self.lo if to is None else to

    KB = 1024
    PERS_END = 80 * KB
    pers = Region(0, PERS_END)
    RP = Region(PERS_END, limit)

    ident = pers.alloc("ident", [128, 128], BF16)
    dft128 = pers.alloc("dft128s", [128, 256], BF16)
    shiftT = pers.alloc("shiftT", [128, 8, 2], F32)
    scale1T = pers.alloc("scale1T", [128, 8, 2], F32)
    gate_bc = pers.alloc("gate_bc", [128, 2, 1024], F32)
    kvg_bc = pers.alloc("kvg_bc", [128, 128], F32)
    qg_bc = pers.alloc("qg_bc", [128, 256], F32)
    neghalf = pers.alloc("neghalf", [128, 1], F32)
    ropek = pers.alloc("ropeks", [128, 16, 2, 32], F32)
    hT_own = pers.alloc("hT_own", [128, 3, 8, 512], BF16)
    zf_own = pers.alloc("zf_own", [128, 3, 4, 512], BF16)
    ckvT = pers.alloc("ckvT", [128, 2816], BF16)
    krT = pers.alloc("krT", [128, 2816], BF16)
    cqT = pers.alloc("cqT", [128, 2, 1536], BF16)
    za_own = pers.alloc("za_own", [128, 4, 1536], BF16)
    stat = pers.alloc("stat", [128, 64], F32)

    PS2 = []
    PSB = []
    PSBH = []
    for i in range(4):
        t = nc.alloc_psum_tensor(f"ps2_{i}", [128, 1024], F32)
        S.reg[t.name] = ('ps', 2 * i)
        tb = t.bitcast(BF16)
        S.reg[tb.name] = ('ps', 2 * i)
        PS2.append(t)
        for hh in range(2):
            PSB.append(t[:, hh * 512:(hh + 1) * 512])
            PSBH.append(tb[:, hh * 1024:(hh + 1) * 1024])

    stat_ctr = [0]

    def newstat():
        i = stat_ctr[0] % 64
        stat_ctr[0] += 1
        return stat[:, i:i + 1]

    def dma(q, out, in_, store=False):
        return S.add(q, lambda e: e.dma_start(out=out, in_=in_), reads=[in_], writes=[out], dma=True, store=store)

    def mm(out, lhsT, rhs, start, stop):
        return S.add('pe', lambda e: e.matmul(out, lhsT=lhsT, rhs=rhs, start=start, stop=stop),
                     reads=[lhsT, rhs], writes=[out])

    def tr(out, in_):
        return S.add('pe', lambda e: e.transpose(out=out, in_=in_, identity=ident[:]), reads=[in_, ident[:]], writes=[out])

    def act(out, in_, func, scale=1.0, bias=0.0, accum=None):
        rd = [in_]
        if not isinstance(scale, float):
            rd.append(scale)
        if not isinstance(bias, float):
            rd.append(bias)
        wr = [out] + ([accum] if accum is not None else [])
        if accum is not None:
            return S.add('act', lambda e: e.activation(out=out, in_=in_, func=func, scale=scale, bias=bias, accum_out=accum),
                         reads=rd, writes=wr)
        return S.add('act', lambda e: e.activation(out=out, in_=in_, func=func, scale=scale, bias=bias), reads=rd, writes=wr)

    def cp(eng, out, in_):
        if eng == 'act':
            return S.add('act', lambda e: e.copy(out=out, in_=in_), reads=[in_], writes=[out])
        return S.add(eng, lambda e: e.tensor_copy(out=out, in_=in_), reads=[in_], writes=[out])

    def tt(eng, out, in0, in1, op):
        return S.add(eng, lambda e: e.tensor_tensor(out=out, in0=in0, in1=in1, op=op), reads=[in0, in1], writes=[out])

    def ts(eng, out, in0, s1, s2, op0, op1=None):
        rd = [in0] + [s for s in (s1, s2) if s is not None and not isinstance(s, float)]
        if op1 is None:
            return S.add(eng, lambda e: e.tensor_scalar(out=out, in0=in0, scalar1=s1, scalar2=None, op0=op0), reads=rd, writes=[out])
        return S.add(eng, lambda e: e.tensor_scalar(out=out, in0=in0, scalar1=s1, scalar2=s2, op0=op0, op1=op1), reads=rd, writes=[out])

    def stt(out, in0, scalar, in1, op0, op1):
        rd = [in0, in1] + ([] if isinstance(scalar, float) else [scalar])
        return S.add('dve', lambda e: e.scalar_tensor_tensor(out=out, in0=in0, scalar=scalar, in1=in1, op0=op0, op1=op1),
                     reads=rd, writes=[out])

    def recip(out, in_):
        return S.add('dve', lambda e: e.reciprocal(out=out, in_=in_), reads=[in_], writes=[out])

    def memset(eng, out, val):
        return S.add(eng, lambda e: e.memset(out, val), writes=[out])

    def rstd_from_ssq(ssq, n):
        v = newstat()
        ts('pool', v, ssq, 1.0 / n, EPS, ALU.mult, ALU.add)
        r = newstat()
        tt('pool', r, v, neghalf[:], ALU.pow)
        return r

    taps = []

    def tap(name, ap_, shape, dt=F32):
        if debug_taps is None or name not in debug_taps:
            return
        o = dout("dbg_" + name, shape, dt)
        dma('sp', o, ap_, store=True)
        taps.append(name)

    def record():
        RP.reset()
        AB = RP.alloc("AB", [128, 20, 4, 256], BF16)
        wA = RP.alloc("wA", [128, 8, 1952], BF16)
        passA_base = RP.cur
        xt = [RP.alloc(f"xt{i}", [128, 1024], F32) for i in range(3)]
        xn = [RP.alloc(f"xn{i}", [128, 1024], BF16) for i in range(2)]
        hT_tmp = [RP.alloc(f"hT_tmp{i}", [128, 8, 512], BF16) for i in range(2)]
        uT = [RP.alloc(f"uT{i}", [128, 4, 512], BF16) for i in range(2)]
        tht = [RP.alloc(f"tht{i}", [128, 512], BF16) for i in range(2)]
        tmf = [RP.alloc(f"tmf{i}", [128, 416], F32) for i in range(2)]
        ckf = [RP.alloc(f"ckf{i}", [128, 128], F32) for i in range(2)]
        ckb = [RP.alloc(f"ckb{i}", [128, 128], BF16) for i in range(4)]
        kst = [RP.alloc(f"kst{i}", [128, 96], BF16) for i in range(4)]
        cqb = [RP.alloc(f"cqb{i}", [128, 256], BF16) for i in range(4)]
        rt1 = RP.alloc("rt1", [128, 32], F32)
        rt2 = RP.alloc("rt2", [128, 32], F32)
        ccf = RP.alloc("ccf", [128, 2, 128], F32)
        ckrf = RP.alloc("ckrf", [128, 2, 32], F32)


        def prep(k):
            T = k
            x_ = xt[k % 3]
            dma('sp', x_[:], xcat[T * 128:(T + 1) * 128, :])
            ssq = newstat()
            xn_ = xn[k % 2]
            act(xn_[:], x_[:], AF.Square, accum=ssq)
            r = rstd_from_ssq(ssq, 1024)
            ts('dve', xn_[:], x_[:], r, None, ALU.mult)

        P0 = Region(RP.lo, RP.lo + 12 * KB)
        cT = P0.alloc("cT", [128, 8, 2], F32)
        bT = P0.alloc("bT", [128, 24], F32)
        gb = P0.alloc("gb", [128, 1024], F32)
        th0 = P0.alloc("th0", [128, 8, 2], F32)
        sc_bf = P0.alloc("sc_bf", [128, 8, 2], BF16)
        screp = P0.alloc("screp", [128, 8, 2, 128], BF16)
        PG = Region(RP.lo + 24 * KB, RP.lo + 40 * KB)
        wada_g = PG.alloc("wada_g", [128, 8, 1024], BF16)
        PS_ = Region(RP.hi - 32 * KB, RP.hi)
        wada = PS_.alloc("wada", [128, 8, 2048], BF16)

        dma('sp', ident[:], identd)
        dma('sp', dft128[:], dft128d)
        dma('sp', cT[:], condT)
        dma('sp', bT[:], b_adaT)
        dma('sp', gb[:], b_ada[2048:3072].partition_broadcast(128))
        dma('sp', kvg_bc[:], kvg.partition_broadcast(128))
        dma('sp', qg_bc[:], qng.partition_broadcast(128))
        dma('sp', ropek[:], ropekd)
        memset('pool', neghalf[:], -0.5)
        for k in range(8):
            for hh in range(2):
                dma('pool', wada[:, k, hh * 1024:(hh + 1) * 1024], w_ada[k * 128:(k + 1) * 128, hh * 1024:(hh + 1) * 1024])

        prep(0)
        prep(1)
        act(th0[:], cT[:], AF.Tanh, scale=0.5)
        stt(sc_bf[:], th0[:], 1.0, cT[:], ALU.add, ALU.mult)
        cp('dve', screp[:], sc_bf[:].unsqueeze(3).to_broadcast([128, 8, 2, 128]))
        psM = PSB[0]
        memset('dve', psM[:, 0:32], 0.0)
        for k in range(8):
            for j in range(16):
                o_ = psM[:, 2 * j:2 * j + 2]
                l_ = wada[:, k, j * 128:(j + 1) * 128]
                r_ = sc_bf[:, k, :]
                S.add('pe', lambda e, o_=o_, l_=l_, r_=r_, k=k: e.matmul(o_, lhsT=l_, rhs=r_, start=False, stop=(k == 7),
                                                                      skip_group_check=True),
                      reads=[l_, r_], writes=[o_])
        psMv = psM[:, 0:32].rearrange("p (j c) -> p j c", c=2)
        stt(shiftT[:], psMv[:, 0:8, :], 0.5, bT[:, 0:8].unsqueeze(2).to_broadcast([128, 8, 2]), ALU.mult, ALU.add)
        stt(scale1T[:], psMv[:, 8:16, :], 0.5, bT[:, 8:16].unsqueeze(2).to_broadcast([128, 8, 2]), ALU.mult, ALU.add)
        ts('dve', scale1T[:], scale1T[:], 1.0, None, ALU.add)
        ts('dve', gb[:], gb[:], 0.5, None, ALU.mult)

        def gate_stage():
            for c in range(2):
                for hh in range(2):
                    pg = PSB[2 + (c * 2 + hh) % 2]
                    for k in range(8):
                        mm(pg[:, :], screp[:, k, c, :], wada_g[:, k, hh * 512:(hh + 1) * 512], k == 0, k == 7)
                    stt(gate_bc[:, c, hh * 512:(hh + 1) * 512], pg[:, :], 0.25, gb[:, hh * 512:(hh + 1) * 512], ALU.mult, ALU.add)
        tap("shiftT", shiftT[:], [128, 8, 2])
        tap("scale1T", scale1T[:], [128, 8, 2])
        tap("gate_bc", gate_bc[:], [128, 2, 1024])

        if STOP_AFTER == 'phase0':
            return
        for (c0_, c1_) in ((0, 512), (512, 1232), (1232, 1952)):
            for k in range(8):
                dma('pool', wA[:, k, c0_:c1_], w_in[k * 128:(k + 1) * 128, c0_:c1_])
        for i in range(4):
            memset('pool', kst[i][:], 0.0)

        def load_gate_cols():
            for k in range(8):
                dma('pool', wada_g[:, k, :], w_ada[k * 128:(k + 1) * 128, 2048:3072])

        OWN = (0, 1, 2)

        def blk_cond(b):
            return 0 if b == 0 else 1

        def hT_of(b):
            return hT_own[:, b] if b in OWN else hT_tmp[b % 2][:]

        tile_ctr = [0]

        def xpose(k):
            b, t = k // 4, k % 4
            c = blk_cond(b)
            hT = hT_of(b)
            xn_ = xn[k % 2]
            pT = PSBH[(0, 4, 5, 7)[k]] if k < 4 else PSBH[0]
            for j in range(8):
                tr(pT[:, j * 128:(j + 1) * 128], xn_[:, j * 128:(j + 1) * 128])
            for j in range(8):
                o = hT[:, j, t * 128:(t + 1) * 128]
                i_ = pT[:, j * 128:(j + 1) * 128]
                if k % 2 == 0 and j < 4:
                    act(o, i_, AF.Identity, scale=scale1T[:, j, c:c + 1], bias=shiftT[:, j, c:c + 1])
                else:
                    ts('dve', o, i_, scale1T[:, j, c:c + 1], shiftT[:, j, c:c + 1], ALU.mult, ALU.add)

        def stage1_units(b):
            def unit(t):
                k = 4 * b + t
                xpose(k)
                if k + 2 < 20:
                    prep(k + 2)
            return [lambda t=t: unit(t) for t in range(4)]

        mmbank = [0]

        def proj_fm(hT, col0, evac):
            p = PSB[1 + mmbank[0] % 3]
            mmbank[0] += 1
            for k in range(8):
                mm(p[:, :], wA[:, k, col0:col0 + 128], hT[:, k, :], k == 0, k == 7)
            evac(p)

        thc = [0]

        def silu2_evac(out):
            def f(p):
                act(out, p[:, :], AF.Silu)
            return f

        stg = [0]
        tm_info = {}
        tmc = [0]

        def stage2_units(b):
            hT = hT_of(b)
            own = b in OWN
            u = uT[b % 2]
            units = []
            for g in range(4):
                units.append(lambda g=g: proj_fm(hT, g * 128, lambda p: cp('act', u[:, g, :], p[:, :])))
            if own:
                for j in range(4):
                    units.append(lambda j=j: proj_fm(hT, 512 + j * 128, silu2_evac(zf_own[:, b, j, :])))
                for j in range(4):
                    units.append(lambda j=j: proj_fm(hT, 1440 + j * 128, silu2_evac(za_own[:, j, b * 512:(b + 1) * 512])))

            def tm_unit(t):
                T = 4 * b + t
                p = PSB[6]
                if own:
                    c0, ncol, oq, ock, okr = 1024, 416, 0, 256, 384
                else:
                    c0, ncol, oq, ock, okr = 1280, 160, None, 0, 128
                for k in range(8):
                    mm(p[:, 0:ncol], hT[:, k, t * 128:(t + 1) * 128], wA[:, k, c0:c0 + ncol], k == 0, k == 7)
                f_ = tmf[tmc[0] % 2]
                tmc[0] += 1
                cp('act', f_[:, 0:ncol], p[:, 0:ncol])
                si = stg[0] % 4
                stg[0] += 1
                s2 = newstat()
                act(ckb[si][:], f_[:, ock:ock + 128], AF.Square, accum=s2)
                r2 = rstd_from_ssq(s2, 128)
                if b == 0:
                    cf = ckf[si % 2]
                    stt(cf[:], f_[:, ock:ock + 128], r2, kvg_bc[:], ALU.mult, ALU.mult)
                    cp('pool', ckb[si][:], cf[:])
                    dma('pool', stc_out[T * 128:(T + 1) * 128, :], cf[:], store=True)
                else:
                    stt(ckb[si][:], f_[:, ock:ock + 128], r2, kvg_bc[:], ALU.mult, ALU.mult)
                if b == 0:
                    cp('pool', kst[si][:, 64:96], f_[:, okr:okr + 32])
                    dma('pool', stk_out[T * 128:(T + 1) * 128, :], f_[:, okr:okr + 32], store=True)
                else:
                    ch = T - 4
                    pk = f_[:, okr:okr + 32]
                    tt('pool', rt1[:], pk, ropek[:, ch, 0, :], ALU.mult)
                    pk3 = pk.rearrange("p (a b c) -> p a b c", a=2, b=2)
                    sn3 = ropek[:, ch, 1, :].rearrange("p (a b c) -> p a b c", a=2, b=2)
                    r23 = rt2[:].rearrange("p (a b c) -> p a b c", a=2, b=2)
                    tt('pool', r23[:, :, 0, :], pk3[:, :, 1, :], sn3[:, :, 0, :], ALU.mult)
                    tt('pool', r23[:, :, 1, :], pk3[:, :, 0, :], sn3[:, :, 1, :], ALU.mult)
                    tt('pool', kst[si][:, 64:96], rt1[:], rt2[:], ALU.add)
                if own:
                    s3 = newstat()
                    act(cqb[si][:], f_[:, oq:oq + 256], AF.Square, accum=s3)
                    r3 = rstd_from_ssq(s3, 256)
                    stt(cqb[si][:], f_[:, oq:oq + 256], r3, qg_bc[:], ALU.mult, ALU.mult)
                tm_info[(b, t)] = si
            for t in range(4):
                units.append(lambda t=t: tm_unit(t))
            return units

        def key_chunk(b, t):
            return 4 * b + t

        def stage3_units(b):
            own = b in OWN
            u = uT[b % 2]

            def unit(t):
                T = 4 * b + t
                for half in range(2):
                    p = PSB[4 + half]
                    for gg in range(2):
                        g = half * 2 + gg
                        mm(p[:, gg * 256:(gg + 1) * 256], u[:, g, t * 128:(t + 1) * 128], dft128[:, :], True, True)
                    cp('dve' if half == 0 else 'act', AB[:, T, 2 * half:2 * half + 2, :],
                       p[:, :].rearrange("p (g m) -> p g m", g=2))
                si = tm_info[(b, t)]
                kc = key_chunk(b, t)
                pt = PSBH[7]
                tr(pt[:, 0:128], ckb[si][:])
                tr(pt[0:96, 128:256], kst[si][:])
                if own:
                    for r_ in range(2):
                        tr(pt[:, 256 + r_ * 128:384 + r_ * 128], cqb[si][:, r_ * 128:(r_ + 1) * 128])
                cp('dve', ckvT[:, kc * 128:(kc + 1) * 128], pt[:, 0:128])
                cp('dve', krT[64:96, kc * 128:(kc + 1) * 128], pt[64:96, 128:256])
                if own:
                    cp('dve', cqT[:, :, T * 128:(T + 1) * 128], pt[:, 256:512].rearrange("p (r t) -> p r t", r=2))
            return [lambda t=t: unit(t) for t in range(4)]

        def cache_stage():
            dma('sp', ccf[:], cckv.rearrange("(c p) r -> p c r", p=128))
            dma('sp', ckrf[:], ckro.rearrange("(c p) r -> p c r", p=128))
            for c_ in range(2):
                si = stg[0] % 4
                stg[0] += 1
                cp('pool', ckb[si][:], ccf[:, c_, :])
                cp('pool', kst[si][:, 64:96], ckrf[:, c_, :])
                pt = PSBH[7]
                tr(pt[:, 0:128], ckb[si][:])
                tr(pt[0:96, 128:256], kst[si][:])
                kc = 20 + c_
                cp('dve', ckvT[:, kc * 128:(kc + 1) * 128], pt[:, 0:128])
                cp('act', krT[64:96, kc * 128:(kc + 1) * 128], pt[64:96, 128:256])

        NB = 5
        for i in range(NB + 2):
            if i == 1:
                load_gate_cols()
            if i == 2:
                gate_stage()
            a_units = stage1_units(i) if i < NB else []
            s3 = stage3_units(i - 2) if 0 <= i - 2 < NB else []
            s2 = stage2_units(i - 1) if 0 <= i - 1 < NB else []
            tmu = s2[-4:] if s2 else []
            pj = s2[:-4] if s2 else []
            b_units = []
            npj = len(pj)
            for t in range(4):
                if s3:
                    b_units.append(s3[t])
                lo_, hi_ = (t * npj) // 4, ((t + 1) * npj) // 4
                b_units += pj[lo_:hi_]
                if tmu and t >= 1:
                    b_units.append(tmu[t - 1])
            if tmu:
                b_units.append(tmu[3])
            na, nb_ = len(a_units), len(b_units)
            if na == 0:
                for u_ in b_units:
                    u_()
            else:
                per = nb_ / na
                done = 0
                for ai, au in enumerate(a_units):
                    au()
                    upto = int(round((ai + 1) * per))
                    while done < upto:
                        b_units[done]()
                        done += 1
                while done < nb_:
                    b_units[done]()
                    done += 1
        cache_stage()
        tap("hT0", hT_own[:, 0], [128, 8, 512], BF16)
        tap("ckvT", ckvT[:], [128, 2816], BF16)
        tap("krT", krT[64:96, :], [32, 2816], BF16)
        tap("cqT", cqT[:], [128, 2, 1536], BF16)
        tap("zf", zf_own[:], [128, 3, 4, 512], BF16)
        tap("za", za_own[:], [128, 4, 1536], BF16)
        tap("AB", AB[:], [128, 20, 4, 256], BF16)

        if STOP_AFTER == 'passA':
            return
        RP.reset(passA_base)
        Ts0 = RP.alloc("Ts0", [128, 2, 8, 512], BF16)
        Tp = RP.alloc("Tp", [128, 2, 2, 256], BF16)
        Ts1 = RP.alloc("Ts1", [128, 2, 8, 512], BF16)
        corr = RP.alloc("corr", [1, 1024], BF16)
        Ts = [Ts0, Ts1]
        four_end = RP.cur
        AW = Region(RP.hi - 16 * KB, RP.hi)
        assert AW.lo >= four_end
        wq = AW.alloc("wq", [128, 2, 1024], BF16)
        wkv = AW.alloc("wkv", [128, 1024], BF16)
        ropeq = AW.alloc("ropeqs", [128, 2, 1024], F32)
        for ab in range(2):
            dma('sp', Ts[0][:, ab], dftsd[0, ab])
        dma('sp', Tp[:], dftpd)
        for ab in range(2):
            dma('sp', Ts[1][:, ab], dftsd[1, ab])
        dma('sp', corr[:], dftcd)
        for c_ in range(8):
            own_ = AB[:, 4 + c_].rearrange("p g (ab m) -> p g ab m", ab=2)
            oth_ = AB[:, 12 + c_].rearrange("p g (ab m) -> p g ab m", ab=2)
            if c_ == 0:
                pass
            tt('dve' if c_ % 2 == 0 else 'pool', own_[:, :, 0, :], own_[:, :, 0, :], oth_[:, :, 0, :], ALU.add)
            tt('dve' if c_ % 2 == 0 else 'pool', own_[:, :, 1, :], own_[:, :, 1, :], oth_[:, :, 1, :], ALU.subtract)
        for r_ in range(2):
            dma('pool', wq[:, r_, :], w_uq[r_ * 128:(r_ + 1) * 128, :])
        dma('pool', wkv[:], w_ukv)
        dma('sp', ropeq[64:96, 0, :], ropeqd[:, 0, :])
        dma('sp', ropeq[96:128, 1, :], ropeqd[:, 1, :])
        fb = [0]
        for s_ in range(2):
            for g in range(4):
                p = PSB[4 + fb[0] % 4]
                fb[0] += 1
                n = 0
                for c_ in range(2):
                    for ab in range(2):
                        mm(p[:, 0:256], AB[:, 2 * s_ + c_, g, ab * 128:(ab + 1) * 128], Tp[:, c_, ab, :], n == 0, n == 3)
                        n += 1
                o = zf_own[:, 0, g, s_ * 256:(s_ + 1) * 256]
                tt('dve', o, p[:, 0:256], o, ALU.mult)
        for kb in range(2):
            for ab in range(2):
                for g in range(4):
                    p = PSB[4 * kb + g]
                    for c_ in range(8):
                        mm(p[:, :], AB[:, 4 + c_, g, ab * 128:(ab + 1) * 128], Ts[kb][:, ab, c_, :],
                           ab == 0 and c_ == 0, False)
            for g in range(4):
                p = PSB[4 * kb + g]
                mm(p[:, :], AB[0:1, 12, g, 0:128], corr[0:1, kb * 512:(kb + 1) * 512], False, True)
                o = zf_own[:, 1 + kb, g, :]
                tt('dve', o, p[:, :], o, ALU.mult)
        tap("fmz", zf_own[:], [128, 3, 4, 512], BF16)

        if STOP_AFTER == 'fourier':
            return
        RP.reset()
        KT = [RP.alloc(f"KT{i}", [128, 2304], BF16) for i in range(2)]
        Vh = [RP.alloc(f"Vh{i}", [128, 18, 128], BF16) for i in range(2)]
        QT = [RP.alloc(f"QT{i}", [128, 1024], BF16) for i in range(2)]
        PT = [RP.alloc(f"PT{i}", [128, 2, 512], BF16) for i in range(3)]
        qr1 = RP.alloc("qr1", [128, 512], F32)
        qr2 = RP.alloc("qr2", [128, 512], F32)
        QX = nc.alloc_sbuf_tensor_at("QX", [128, 2048], BF16, offset=base0 + S.reg[qr1.name][1])
        S.reg[QX.name] = ('sb', S.reg[qr1.name][1])
        assert S.reg[qr2.name][1] == S.reg[qr1.name][1] + 2048
        rc = [RP.alloc(f"rc{i}", [128, 512], F32) for i in range(2)]
        on_ = [RP.alloc(f"on{i}", [128, 512], F32) for i in range(2)]
        att_end = RP.cur
        MT_END = att_end + 2 * KB
        RP.reset(MT_END)
        wC = RP.alloc("wC", [128, 8, 2048], BF16)
        wf = RP.alloc("wf", [128, 4, 1024], BF16)
        wa = RP.alloc("wa", [128, 4, 1024], BF16)
        wo = RP.alloc("wo", [128, 8, 1024], BF16)
        fng_bc = RP.alloc("fng_bc", [128, 1024], F32)
        assert RP.cur <= AW.lo, (RP.cur, AW.lo)
        for k in range(8):
            for hh in range(2):
                dma('pool', wC[:, k, hh * 1024:(hh + 1) * 1024], w_in[k * 128:(k + 1) * 128, 1952 + hh * 1024:1952 + (hh + 1) * 1024])
        for g in range(4):
            dma('pool', wf[:, g, :], w_f_out[g * 128:(g + 1) * 128, :])
            dma('pool', wa[:, g, :], w_a_out[g * 128:(g + 1) * 128, :])
        for k in range(8):
            dma('pool', wo[:, k, :], w_out[k * 128:(k + 1) * 128, :])
        dma('sp', fng_bc[:], fng.partition_broadcast(128))

        slot_ctr = [0]
        obk = [0]

        ub = [0]

        def attn_setup_units(h, key_chunks, seqs, rope, st):
            sl = slot_ctr[0] % 2
            slot_ctr[0] += 1
            if rope:
                kt, vh, qt = KT[sl], Vh[sl], QT[sl]
            else:
                kt = KT[h // 4][:, (h % 4) * 512:(h % 4 + 1) * 512]
                vh = Vh[h // 4][:, (h % 4) * 4:(h % 4 + 1) * 4, :]
                if h < 2:
                    qt = QT[0][:, h * 512:(h + 1) * 512]
                elif h < 6:
                    qt = QX[:, (h - 2) * 512:(h - 1) * 512]
                else:
                    qt = QT[1][:, (h - 6) * 512:(h - 5) * 512]
            odd = h % 2
            nk = len(key_chunks)
            kc0 = key_chunks[0]
            ncols = nk * 128
            q_lo = min(s_[2] for s_ in seqs)
            q_hi = max(s_[2] + s_[3] for s_ in seqs)
            st.update(kt=kt, vh=vh, qt=qt, odd=odd, q_lo=q_lo, h=h)
            units = []
            vo = 64 if odd else 0
            oo = 0 if odd else 64

            def u0():
                cp('dve', kt[64:96, 0:ncols], krT[64:96, kc0 * 128:kc0 * 128 + ncols])
                if st['gi'] < 2 or not rope:
                    memset('pool', vh[:, :, oo:oo + 64], 1.0)
            if st['gi'] in (0, 1) or not rope:
                units.append(u0)

            def uk(c0):
                n = min(512, ncols - c0)
                p = PSB[ub[0] % 2]
                ub[0] += 1
                mm(p[:, 0:n], wkv[:, h * 128:(h + 1) * 128], ckvT[:, kc0 * 128 + c0:kc0 * 128 + c0 + n], True, True)
                cp('dve', kt[0:64, c0:c0 + n], p[0:64, 0:n])
            for c0 in range(0, ncols, 512):
                units.append(lambda c0=c0: uk(c0))

            def uv(c0):
                n = min(8, nk - c0)
                p = PSB[ub[0] % 2]
                ub[0] += 1
                for c_ in range(n):
                    mm(p[:, c_ * 64:(c_ + 1) * 64], ckvT[:, (kc0 + c0 + c_) * 128:(kc0 + c0 + c_ + 1) * 128],
                       wkv[:, h * 128 + 64:h * 128 + 128], True, True)
                cp('dve', vh[:, c0:c0 + n, vo:vo + 64], p[:, 0:n * 64].rearrange("p (c d) -> p c d", d=64))
            for c0 in range(0, nk, 8):
                units.append(lambda c0=c0: uv(c0))

            def uq(c0):
                n = min(512, q_hi - c0)
                lo = c0 - q_lo
                p = PSB[ub[0] % 2]
                ub[0] += 1
                if rope:
                    for r_ in range(2):
                        mm(p[:, 0:n], wq[:, r_, h * 128:(h + 1) * 128], cqT[:, r_, c0:c0 + n], r_ == 0, r_ == 1)
                    rq0 = c0 - 512
                    cp('dve', qt[0:64, lo:lo + n], p[0:64, 0:n])
                    tt('dve', qr1[64:96, 0:n], p[64:96, 0:n], ropeq[64:96, 0, rq0:rq0 + n], ALU.mult)
                    tt('dve', qr2[64:96, 0:n], p[96:128, 0:n], ropeq[96:128, 1, rq0:rq0 + n], ALU.mult)
                    tt('pool', qt[64:96, lo:lo + n], qr1[64:96, 0:n], qr2[64:96, 0:n], ALU.add)
                else:
                    for r_ in range(2):
                        mm(p[0:96, 0:n], wq[:, r_, h * 128:h * 128 + 96], cqT[:, r_, c0:c0 + n], r_ == 0, r_ == 1)
                    cp('dve', qt[0:96, lo:lo + n], p[0:96, 0:n])
            for c0 in range(q_lo, q_hi, 512):
                units.append(lambda c0=c0: uq(c0))
            return units

        jobs = []
        hooks = {}
        groups = [(h, list(range(4, 22)), [(0, 18, 512, 1024)], True) for h in range(8)] + \
                 [(h, [0, 1, 2, 3], [(0, 2, 0, 256), (2, 4, 256, 256)], False) for h in range(8)]
        states = {gi: {'gi': gi} for gi in range(len(groups))}
        grp_jobs = []
        for gi, (h, key_chunks, seqs, rope) in enumerate(groups):
            first_job = len(jobs)
            for (k_lo, k_hi, q0, nq) in seqs:
                for qb0 in range(0, nq, 512):
                    n = min(512, nq - qb0)
                    og = obk[0]
                    obk[0] += 1
                    for kc in range(k_lo, k_hi, 2):
                        jobs.append(dict(gi=gi, kcs=list(range(kc, min(kc + 2, k_hi))), q0=q0, qb0=qb0, n=n, og=og,
                                         first=(kc == k_lo), last=(kc + 2 >= k_hi)))
            grp_jobs.append((first_job, len(jobs) - first_job))

        def add_hook(ji, f):
            hooks.setdefault(ji, []).append(f)

        first_units = None
        prompt_units = []
        for gi in range(len(groups)):
            units = attn_setup_units(*groups[gi], states[gi])
            if gi == 0:
                first_units = units
                continue
            if not groups[gi][3]:
                prompt_units.append(units)
                continue
            fj, nj = grp_jobs[gi - 1]
            span = max(nj - 1, 1)
            for k_, u_ in enumerate(units):
                ji = fj + min(nj - 1, 1 + (k_ * span) // len(units)) if nj > 1 else fj
                add_hook(ji, u_)

        def issue_S(ji):
            j = jobs[ji]
            st = states[j['gi']]
            bank = PS2[1 + ji % 2]
            ql = j['q0'] - st['q_lo'] + j['qb0']
            n = j['n']
            for i, kc in enumerate(j['kcs']):
                mm(bank[:, i * 512:i * 512 + n], st['kt'][0:96, kc * 128:(kc + 1) * 128], st['qt'][0:96, ql:ql + n], True, True)

        deferred = []

        def issue_rest(ji):
            j = jobs[ji]
            st = states[j['gi']]
            bank = PS2[1 + ji % 2]
            n = j['n']
            L = len(j['kcs'])
            pt_ = PT[ji % 3]
            act(pt_[:, 0:L, 0:n], bank[:, :].rearrange("p (c n) -> p c n", c=2)[:, 0:L, 0:n], AF.Exp, scale=SM_SCALE)
            samp = groups[j['gi']][3]
            po = PSB[6 + j['og'] % 2]
            for i, kc in enumerate(j['kcs']):
                mm(po[:, 0:n], st['vh'][:, kc, :], pt_[:, i, 0:n], j['first'] and i == 0, j['last'] and i == L - 1)
            if j['last']:
                odd = st['odd']
                h = st['h']
                pb = 64 if odd else 0
                db = 0 if odd else 64
                if samp:
                    rc_ = rc[j['og'] % 2]
                    on = on_[j['og'] % 2]
                else:
                    co = ((j['og'] // 2) % 2) * 256
                    rc_ = rc[j['og'] % 2][:, co:co + 256]
                    on = on_[j['og'] % 2][:, co:co + 256]
                zo = za_own[pb:pb + 64, h // 2, j['q0'] + j['qb0']:j['q0'] + j['qb0'] + n]
                if groups[j['gi']][3]:
                    hn = n // 2
                    deferred.append(lambda: recip(rc_[pb:pb + 64, 0:hn], po[db:db + 64, 0:hn]))
                    deferred.append(lambda: recip(rc_[pb:pb + 64, hn:n], po[db:db + 64, hn:n]))

                    def fin():
                        tt('dve', on[pb:pb + 64, 0:n], po[pb:pb + 64, 0:n], rc_[pb:pb + 64, 0:n], ALU.mult)
                        tt('pool', zo, on[pb:pb + 64, 0:n], zo, ALU.mult)
                    deferred.append(fin)
                else:
                    recip(rc_[pb:pb + 64, 0:n], po[db:db + 64, 0:n])
                    tt('dve', on[pb:pb + 64, 0:n], po[pb:pb + 64, 0:n], rc_[pb:pb + 64, 0:n], ALU.mult)
                    tt('pool', zo, on[pb:pb + 64, 0:n], zo, ALU.mult)

        n_samp = grp_jobs[8][0]

        def run_pipeline(j0, j1):
            issue_S(j0)
            for ji in range(j0, j1):
                for f_ in hooks.get(ji, []):
                    f_()
                if ji + 1 < j1:
                    issue_S(ji + 1)
                if deferred:
                    deferred.pop(0)()
                issue_rest(ji)
            while deferred:
                deferred.pop(0)()

        early_units, late_units = [], []
        for k_ in range(3):
            for h_ in range(8):
                (early_units if h_ < 4 else late_units).append(prompt_units[h_][k_])
        for h_ in range(8):
            (early_units if h_ < 6 else late_units).append(prompt_units[h_][3])
        fj7, nj7 = grp_jobs[7]
        for i_, u_ in enumerate(early_units):
            add_hook(fj7 + 1 + (i_ * (nj7 - 3)) // len(early_units), u_)
        for u_ in first_units:
            u_()
        run_pipeline(0, n_samp)
        for i_, u_ in enumerate(late_units):
            add_hook(n_samp + (i_ * 7) // len(late_units), u_)
        run_pipeline(n_samp, len(jobs))
        tap("attn", za_own[:], [128, 4, 1536], BF16)

        if STOP_AFTER == 'attention':
            return
        MT = Region(RP.lo, MT_END)
        mT = [MT.alloc(f"mT{i}", [128, 8, 512], BF16) for i in range(2)]
        tg = [MT.alloc(f"tg{i}", [128, 512], BF16) for i in range(2)]
        t1 = [MT.alloc(f"t1{i}", [128, 512], F32) for i in range(2)]
        t2 = [MT.alloc(f"t2{i}", [128, 512], F32) for i in range(2)]
        xr = [MT.alloc(f"xr{i}", [128, 1024], F32) for i in range(2)]
        xw = [MT.alloc(f"xw{i}", [128, 1024], F32) for i in range(2)]

        mc = [0]
        for T0 in range(2):
            dma('sp', xr[T0][:], xcat[T0 * 128:(T0 + 1) * 128, :])
        def jiter(b, j):
            m_ = mT[b % 2]
            i_ = mc[0] % 2
            mc[0] += 1
            pg = PSB[0]
            for k in range(8):
                mm(pg[:, :], wC[:, k, j * 128:(j + 1) * 128], hT_own[:, b, k, :], k == 0, k == 7)
            act(tg[0][:], pg[:, :], AF.Tanh, scale=0.5)
            py = PSB[2]
            for g in range(4):
                mm(py[:, :], wf[:, g, j * 128:(j + 1) * 128], zf_own[:, b, g, :], g == 0, g == 3)
            stt(t1[i_][:], tg[0][:], 1.0, py[:, :], ALU.add, ALU.mult)
            pg2 = PSB[1]
            for k in range(8):
                mm(pg2[:, :], wC[:, k, 1024 + j * 128:1024 + (j + 1) * 128], hT_own[:, b, k, :], k == 0, k == 7)
            act(tg[1][:], pg2[:, :], AF.Tanh, scale=0.5)
            py2 = PSB[3]
            for g in range(4):
                mm(py2[:, :], wa[:, g, j * 128:(j + 1) * 128], za_own[:, g, b * 512:(b + 1) * 512], g == 0, g == 3)
            stt(t2[i_][:], tg[1][:], 1.0, py2[:, :], ALU.add, ALU.mult)
            tt('dve', m_[:, j, :], t1[i_][:], t2[i_][:], ALU.add)

        def outproj(b, t):
            c = blk_cond(b)
            m_ = mT[b % 2]
            T = 4 * b + t
            x_ = xr[T % 2]
            w_ = xw[T % 2]
            for hh in range(2):
                po = PSB[4 + 2 * (T % 2) + hh]
                for k in range(8):
                    mm(po[:, :], m_[:, k, t * 128:(t + 1) * 128], wo[:, k, hh * 512:(hh + 1) * 512], k == 0, k == 7)
                tt('dve', w_[:, hh * 512:(hh + 1) * 512], po[:, :], gate_bc[:, c, hh * 512:(hh + 1) * 512], ALU.mult)
            tt('dve', w_[:], w_[:], x_[:], ALU.add)
            ssq = newstat()
            act(x_[:], w_[:], AF.Square, accum=ssq)
            if T + 2 < 12:
                dma('sp', xr[T % 2][:], xcat[(T + 2) * 128:(T + 3) * 128, :])
            r = rstd_from_ssq(ssq, 1024)
            stt(w_[:], w_[:], r, fng_bc[:], ALU.mult, ALU.mult)
            dma('sp', y_out[T * 128:(T + 1) * 128, :], w_[:], store=True)

        for j in range(8):
            jiter(0, j)
        for b in range(3):
            for t in range(4):
                if b + 1 < 3:
                    jiter(b + 1, 2 * t)
                    jiter(b + 1, 2 * t + 1)
                outproj(b, t)

    record()

    S.finalize()
    from contextlib import ExitStack
    with ExitStack() as es:
        esem = {e: es.enter_context(nc.semaphore("s_" + e)) for e in ('pe', 'act', 'dve', 'pool')}
        lsem = {}
        for q, n in NL.items():
            for i in range(n):
                lsem[(q, i)] = es.enter_context(nc.semaphore(f"l_{q}{i}"))
        block = es.enter_context(nc.Block())

        @block.tensor
        def _(e):
            S.emit('pe', e, esem, lsem)

        @block.scalar
        def _(e):
            S.emit('act', e, esem, lsem)

        @block.vector
        def _(e):
            S.emit('dve', e, esem, lsem)

        @block.gpsimd
        def _(e):
            S.emit('pool', e, esem, lsem)

        @block.sync
        def _(e):
            S.emit('sp', e, esem, lsem)
    return nc, taps


def _bf16(a):
    return np.ascontiguousarray(a.astype(np.float32)).astype(ml_dtypes.bfloat16)


_TAB_CACHE = {}


def _tables(hf):
    if hf in _TAB_CACHE:
        return _TAB_CACHE[hf]
    own = np.arange(hf * 1024, hf * 1024 + 1024)
    oth = (2048 - own) % 2048
    oth[0] = 1024 if hf == 0 else 0
    assert sorted(oth.tolist()) == list(range((1 - hf) * 1024, (1 - hf) * 1024 + 1024))
    pos = np.concatenate([own, oth])
    kn = (own[:, None].astype(np.int64) * own[None, :].astype(np.int64)) % 2048
    ang = 2.0 * np.pi * kn.astype(np.float64) / 2048.0
    sc = 1.0 / np.sqrt(2048.0 * 128.0)
    tab = np.stack([np.cos(ang) * sc, -np.sin(ang) * sc])
    tab = tab.reshape(2, 8, 128, 2, 512).transpose(3, 0, 2, 1, 4)
    dfts = _bf16(tab)
    a_o = 2.0 * np.pi * ((int(oth[0]) * own.astype(np.int64)) % 2048) / 2048.0
    a_w = 2.0 * np.pi * ((int(own[0]) * own.astype(np.int64)) % 2048) / 2048.0
    dftc = _bf16(((np.cos(a_o) - np.cos(a_w)) * sc)[None, :])
    half = 16
    inv = 10000.0 ** (-np.arange(0, half, 2, dtype=np.float64) / half)
    row = (pos // 64).astype(np.float64)
    col = (pos % 64).astype(np.float64)
    ar = row[:, None] * inv
    ac = col[:, None] * inv
    ang_r = np.concatenate([ar, ar, ac, ac], axis=-1)
    sgn = np.array([-1.0] * 8 + [1.0] * 8 + [-1.0] * 8 + [1.0] * 8)
    cs = np.stack([np.cos(ang_r), np.sin(ang_r) * sgn], axis=1)
    ropek = cs.reshape(16, 128, 2, 32).transpose(1, 0, 2, 3).astype(np.float32)
    ropeq = cs[:1024].transpose(2, 1, 0).astype(np.float32)
    _TAB_CACHE[hf] = (dfts, np.ascontiguousarray(ropek), np.ascontiguousarray(ropeq), dftc, oth)
    return _TAB_CACHE[hf]


def _const_tables():
    c = np.arange(128)
    ang = 2.0 * np.pi * ((c[:, None] * c[None, :]) % 128) / 128.0
    dft128 = _bf16(np.concatenate([np.cos(ang), np.sin(ang)], axis=1))
    n = np.arange(256)
    angp = 2.0 * np.pi * ((n[:, None] * n[None, :]) % 256) / 256.0
    sc = 1.0 / np.sqrt(256.0 * 128.0)
    tp = np.stack([np.cos(angp) * sc, -np.sin(angp) * sc])
    tp = tp.reshape(2, 2, 128, 256).transpose(2, 1, 0, 3)
    ident = np.eye(128, dtype=np.float32).astype(ml_dtypes.bfloat16)
    return dft128, _bf16(tp), ident


_PROG = {}


def _make_in_maps(x_prompt, x_sample, cache_ckv, cache_krope, c, c_ctx, w_ada, b_ada, w_in,
                  q_norm_g, w_uq, kv_norm_g, w_ukv, w_f_out, w_a_out, w_out, final_norm_g):
    f = lambda a: np.ascontiguousarray(np.asarray(a, dtype=np.float32))
    x_prompt, x_sample = f(x_prompt), f(x_sample)
    dft128, dftp, ident = _const_tables()
    perm = np.array(list(range(8, 16)) + list(range(0, 8)) + list(range(24, 32)) + list(range(16, 24)))
    w_uq0 = f(w_uq)[0]
    w3 = w_uq0.reshape(256, 8, 96)
    w_uq128 = np.ascontiguousarray(np.concatenate([w3, w3[:, :, 64:][:, :, perm]], axis=2).reshape(256, 1024))
    shared = dict(
        b_adaT=np.ascontiguousarray(f(b_ada)[0].reshape(24, 128).T), b_ada=f(b_ada)[0], w_ada=f(w_ada)[0],
        w_in=f(w_in)[0], qng=f(q_norm_g)[0], kvg=f(kv_norm_g)[0], w_uq128=w_uq128, w_ukv=f(w_ukv)[0],
        w_f_out=f(w_f_out)[0], w_a_out=f(w_a_out)[0], w_out=f(w_out)[0], fng=f(final_norm_g),
        ident=ident, dft128=dft128, dftp=dftp)
    in_maps = []
    for i in range(NCORES):
        s, hf = i // 2, i % 2
        dfts, ropek, ropeq, dftc, oth_idx = _tables(hf)
        own = x_sample[s, hf * 1024:(hf + 1) * 1024]
        oth = x_sample[s][oth_idx]
        xcat = np.concatenate([x_prompt[2 * i], x_prompt[2 * i + 1], own, oth], axis=0)
        cond = np.stack([f(c_ctx), f(c)[s]])
        condT = np.ascontiguousarray(cond.reshape(2, 8, 128).transpose(2, 1, 0))
        m = dict(shared)
        m.update(xcat=np.ascontiguousarray(xcat), cckv=f(cache_ckv)[s, 0], ckro=f(cache_krope)[s, 0],
                 condT=condT, dfts=dfts, dftc=dftc, ropek=ropek, ropeq=ropeq)
        in_maps.append(m)
    return in_maps


def kernel(x_prompt, x_sample, cache_ckv, cache_krope, c, c_ctx, w_ada, b_ada, w_in,
           q_norm_g, w_uq, kv_norm_g, w_ukv, w_f_out, w_a_out, w_out, final_norm_g, _debug_taps=None):
    in_maps = _make_in_maps(x_prompt, x_sample, cache_ckv, cache_krope, c, c_ctx, w_ada, b_ada, w_in,
                            q_norm_g, w_uq, kv_norm_g, w_ukv, w_f_out, w_a_out, w_out, final_norm_g)
    nc, taps = build_program(_debug_taps)
    res = run_bass_kernel_spmd(nc, in_maps, core_ids=list(range(NCORES)))
    y_prompt = np.zeros((16, 256, D), np.float32)
    y_sample = np.zeros((4, 2048, D), np.float32)
    st_ckv = np.zeros((16, 1, 256, 128), np.float32)
    st_kr = np.zeros((16, 1, 256, 32), np.float32)
    for i in range(NCORES):
        r = res.results[i]
        s, hf = i // 2, i % 2
        y = np.asarray(r["y"], dtype=np.float32)
        y_prompt[2 * i:2 * i + 2] = y[0:512].reshape(2, 256, D)
        y_sample[s, hf * 1024:(hf + 1) * 1024] = y[512:1536]
        st_ckv[2 * i:2 * i + 2, 0] = np.asarray(r["stc"], dtype=np.float32).reshape(2, 256, 128)
        st_kr[2 * i:2 * i + 2, 0] = np.asarray(r["stk"], dtype=np.float32).reshape(2, 256, 32)
    if _debug_taps is not None:
        return (y_prompt, y_sample, st_ckv, st_kr), [{k: r[k] for k in r if k.startswith("dbg_")} for r in res.results]
    return (y_prompt, y_sample, st_ckv, st_kr)
```
